# Optimizing a Trainium2 kernel written in Bass

```python
import jax, jax.numpy as jnp
from jax import lax
import numpy as np

D_MODEL = 1024
BATCH = 4
SEQ = 4096
DEPTH = 2

CHUNK = 64
LEFT_CHUNKS = 8
BAND = (LEFT_CHUNKS + 1) * CHUNK
ATT_HEADS = 8
ATT_HEAD_DIM = 64
ATT_WIDTH = ATT_HEADS * ATT_HEAD_DIM
MAX_REL_DIST = 128
SGU_CHUNK = 128
SGU_GROUPS = 4
SGU_WIDTH = 512
SGU_GROUP_DIM = SGU_WIDTH // SGU_GROUPS
N_BRANCHES = 2
IN_WIDTH = 3 * ATT_WIDTH + 2 * SGU_WIDTH + N_BRANCHES * D_MODEL
D_FF = -(-(8 * D_MODEL) // (3 * 256)) * 256
EPS = 1e-6

kernel_name = "hybrid_chunk_attn_sgu_block"


def rmsnorm(x, g):
    xf = x.astype(jnp.float32)
    xf = xf * lax.rsqrt(jnp.mean(xf * xf, axis=-1, keepdims=True) + EPS)
    return (xf * g.astype(jnp.float32)).astype(x.dtype)


def layernorm(x, g, b):
    xf = x.astype(jnp.float32)
    mu = jnp.mean(xf, axis=-1, keepdims=True)
    var = jnp.mean(jnp.square(xf - mu), axis=-1, keepdims=True)
    y = (xf - mu) * lax.rsqrt(var + EPS) * g.astype(jnp.float32) + b.astype(jnp.float32)
    return y.astype(x.dtype)


def chunked_rel_attention(q, k, v, rel_bias):
    B, S, H, Dh = q.shape
    nc = S // CHUNK
    pad = LEFT_CHUNKS * CHUNK
    qc = q.reshape(B, nc, CHUNK, H, Dh)
    kp = jnp.pad(k, ((0, 0), (pad, 0), (0, 0), (0, 0)))
    vp = jnp.pad(v, ((0, 0), (pad, 0), (0, 0), (0, 0)))
    band_idx = jnp.arange(nc)[:, None] * CHUNK + jnp.arange(BAND)[None, :]
    kb = kp[:, band_idx]
    vb = vp[:, band_idx]
    scores = jnp.einsum('bcqhd,bckhd->bhcqk', qc, kb).astype(jnp.float32) * (Dh ** -0.5)
    q_pos = pad + jnp.arange(CHUNK)
    k_pos = jnp.arange(BAND)
    rel = jnp.clip(q_pos[:, None] - k_pos[None, :], -MAX_REL_DIST, MAX_REL_DIST) + MAX_REL_DIST
    bias = rel_bias.astype(jnp.float32)[:, rel]
    scores = scores + bias[None, :, None, :, :]
    key_chunk = jnp.arange(nc)[:, None] - LEFT_CHUNKS + (k_pos // CHUNK)[None, :]
    valid = key_chunk >= 0
    scores = jnp.where(valid[None, None, :, None, :], scores, jnp.float32(-1e30))
    probs = jax.nn.softmax(scores, axis=-1).astype(v.dtype)
    out = jnp.einsum('bhcqk,bckhd->bcqhd', probs, vb)
    return out.reshape(B, S, H * Dh)


def spatial_gating(u, v, ln_g, ln_b, w_s, b_s):
    B, S, _ = v.shape
    ng = S // SGU_CHUNK
    v = layernorm(v, ln_g, ln_b)
    vg = v.reshape(B, ng, SGU_CHUNK, SGU_GROUPS, SGU_GROUP_DIM)
    causal = jnp.tril(jnp.ones((SGU_CHUNK, SGU_CHUNK), dtype=bool))
    w = jnp.where(causal[None], w_s, jnp.zeros_like(w_s))
    mixed = jnp.einsum('gts,bnsgd->bntgd', w, vg) + b_s.T[None, None, :, :, None]
    return u * mixed.reshape(B, S, SGU_WIDTH)


def setup_inputs(seed: int = 0) -> dict:
    key = jax.random.key(seed)
    ks = jax.random.split(key, 16)
    f32 = jnp.float32

    def nrm(k, shape, scale):
        return jax.random.normal(k, shape, f32) * scale

    x = nrm(ks[0], (BATCH, SEQ, D_MODEL), 1.0)
    norm_mix = 1.0 + nrm(ks[1], (DEPTH, D_MODEL), 0.02)
    w_in = nrm(ks[2], (DEPTH, D_MODEL, IN_WIDTH), D_MODEL ** -0.5)
    att_rel_bias = nrm(ks[3], (DEPTH, ATT_HEADS, 2 * MAX_REL_DIST + 1), 0.1)
    sgu_norm_gain = 1.0 + nrm(ks[4], (DEPTH, SGU_WIDTH), 0.02)
    sgu_norm_bias = nrm(ks[5], (DEPTH, SGU_WIDTH), 0.02)
    sgu_w = nrm(ks[6], (DEPTH, SGU_GROUPS, SGU_CHUNK, SGU_CHUNK), 0.5 * SGU_CHUNK ** -0.5)
    sgu_b = 1.0 + nrm(ks[7], (DEPTH, SGU_GROUPS, SGU_CHUNK), 0.02)
    w_br_att = nrm(ks[8], (DEPTH, ATT_WIDTH, D_MODEL), ATT_WIDTH ** -0.5)
    w_br_sgu = nrm(ks[9], (DEPTH, SGU_WIDTH, D_MODEL), SGU_WIDTH ** -0.5)
    b_gate = nrm(ks[10], (DEPTH, N_BRANCHES, D_MODEL), 0.02)
    w_out = nrm(ks[11], (DEPTH, D_MODEL, D_MODEL), D_MODEL ** -0.5)
    norm_ffn = 1.0 + nrm(ks[12], (DEPTH, D_MODEL), 0.02)
    w_ffn_in = nrm(ks[13], (DEPTH, D_MODEL, 2 * D_FF), D_MODEL ** -0.5)
    w_ffn_out = nrm(ks[14], (DEPTH, D_FF, D_MODEL), D_FF ** -0.5)
    norm_final = 1.0 + nrm(ks[15], (D_MODEL,), 0.02)
    return {"x": x, "norm_mix": norm_mix, "w_in": w_in, "att_rel_bias": att_rel_bias,
            "sgu_norm_gain": sgu_norm_gain, "sgu_norm_bias": sgu_norm_bias,
            "sgu_w": sgu_w, "sgu_b": sgu_b, "w_br_att": w_br_att, "w_br_sgu": w_br_sgu,
            "b_gate": b_gate, "w_out": w_out, "norm_ffn": norm_ffn,
            "w_ffn_in": w_ffn_in, "w_ffn_out": w_ffn_out, "norm_final": norm_final}


def reference(x, norm_mix, w_in, att_rel_bias, sgu_norm_gain, sgu_norm_bias, sgu_w, sgu_b,
              w_br_att, w_br_sgu, b_gate, w_out, norm_ffn, w_ffn_in, w_ffn_out, norm_final):
    B, S, D = x.shape
    h = x
    for l in range(DEPTH):
        xn = rmsnorm(h, norm_mix[l])
        proj = xn @ w_in[l]
        q, k, v, u, vs, gate_logits = jnp.split(
            proj, np.cumsum([ATT_WIDTH, ATT_WIDTH, ATT_WIDTH, SGU_WIDTH, SGU_WIDTH]).tolist(), axis=-1)
        q = q.reshape(B, S, ATT_HEADS, ATT_HEAD_DIM)
        k = k.reshape(B, S, ATT_HEADS, ATT_HEAD_DIM)
        v = v.reshape(B, S, ATT_HEADS, ATT_HEAD_DIM)
        att = chunked_rel_attention(q, k, v, att_rel_bias[l])
        sgu = spatial_gating(jax.nn.gelu(u), jax.nn.gelu(vs), sgu_norm_gain[l], sgu_norm_bias[l],
                             sgu_w[l], sgu_b[l])
        br_att = att @ w_br_att[l]
        br_sgu = sgu @ w_br_sgu[l]
        gates = jax.nn.sigmoid(gate_logits.reshape(B, S, N_BRANCHES, D) + b_gate[l])
        merged = gates[:, :, 0] * br_att + gates[:, :, 1] * br_sgu
        h = h + merged @ w_out[l]
        hn = rmsnorm(h, norm_ffn[l])
        g, up = jnp.split(hn @ w_ffn_in[l], 2, axis=-1)
        h = h + (jax.nn.silu(g) * up) @ w_ffn_out[l]
    return rmsnorm(h, norm_final)
```

```python
import contextlib
import numpy as np
import concourse.bass as bass
import concourse.mybir as mybir
from concourse.bass_utils import run_bass_kernel_spmd

F32 = mybir.dt.float32
BF16 = mybir.dt.bfloat16
AF = mybir.ActivationFunctionType
ALU = mybir.AluOpType

D = 1024
DFF = 2816
NFC = 22
NBLK = 24
NSB = 6
NS = 6
EPS = 1e-6
NEG = -30000.0
ATT_SKEW = 2

ENGS = ("pe", "act", "dve", "pool", "sp")


class Op:
    __slots__ = ("eng", "idx", "fn", "waits", "signal", "ev_sem", "ev_val", "is_dma", "sigidx")

    def __init__(self, eng, idx, fn, is_dma):
        self.eng = eng
        self.idx = idx
        self.fn = fn
        self.waits = []
        self.signal = False
        self.is_dma = is_dma
        self.ev_sem = None
        self.ev_val = None
        self.sigidx = None


class Sched:
    def __init__(self):
        self.ops = {e: [] for e in ENGS}
        self.last_w = {}
        self.readers = {}
        self.dma_counts = {}
        self.waited = {e: {} for e in ENGS}
        self.final_waits = {}
        self.group_ops = {}

    def _add_dep(self, op, dep):
        if dep is None or dep is op:
            return
        if dep.is_dma:
            key = ("dma", dep.ev_sem)
            prev = self.waited[op.eng].get(key, 0)
            if dep.ev_val <= prev:
                return
            self.waited[op.eng][key] = dep.ev_val
            op.waits.append((dep.ev_sem, dep.ev_val))
            return
        if dep.eng == op.eng:
            return
        prev = self.waited[op.eng].get(dep.eng, -1)
        if dep.idx <= prev:
            return
        self.waited[op.eng][dep.eng] = dep.idx
        dep.signal = True
        op.waits.append(dep)

    def op(self, eng, fn, reads=(), writes=(), dma_sem=None):
        is_dma = dma_sem is not None
        excl = [b for b in reads if isinstance(b, tuple) and b[0] == "ps"]
        if excl:
            reads = [b for b in reads if not (isinstance(b, tuple) and b[0] == "ps")]
            writes = list(writes) + excl
        o = Op(eng, len(self.ops[eng]), fn, is_dma)
        if is_dma:
            c = self.dma_counts.get(dma_sem, 0) + 16
            self.dma_counts[dma_sem] = c
            o.ev_sem = dma_sem
            o.ev_val = c
            self.group_ops.setdefault(dma_sem, []).append(o)
        for b in reads:
            w = self.last_w.get(b)
            if w is not None:
                if (not w.is_dma) and w.eng == eng and not is_dma:
                    if eng != "pe" and o.idx - w.idx <= 2:
                        w.signal = True
                        if w not in o.waits:
                            o.waits.append(w)
                else:
                    self._add_dep(o, w)
        for b in writes:
            w = self.last_w.get(b)
            if w is not None:
                self._add_dep(o, w)
            for r in self.readers.get(b, {}).values():
                self._add_dep(o, r)
        for b in reads:
            self.readers.setdefault(b, {})[(eng, o.ev_sem) if is_dma else eng] = o
        for b in writes:
            self.last_w[b] = o
            self.readers[b] = {}
        self.ops[eng].append(o)
        return o

    def dma_group_finalize(self, key):
        tot = self.dma_counts.get(key, 0)
        for o in self.group_ops.get(key, []):
            o.ev_val = tot
        self.group_ops[key] = []

    def emit(self, block, sems, dma_sems):
        for e in ENGS:
            n = 0
            for o in self.ops[e]:
                if (not o.is_dma) and o.signal:
                    n += 1
                    o.sigidx = n

        def run(engname, engine):
            for o in self.ops[engname]:
                for d in o.waits:
                    if isinstance(d, tuple):
                        engine.wait_ge(dma_sems[d[0]], d[1])
                    else:
                        engine.wait_ge(sems[d.eng], d.sigidx)
                ins = o.fn(engine)
                if o.is_dma:
                    ins.then_inc(dma_sems[o.ev_sem], 16)
                elif o.signal:
                    ins.then_inc(sems[o.eng], 1)
            for key, cnt in self.final_waits.get(engname, []):
                engine.wait_ge(dma_sems[key], cnt)

        block.tensor(lambda e: run("pe", e))
        block.scalar(lambda e: run("act", e))
        block.vector(lambda e: run("dve", e))
        block.gpsimd(lambda e: run("pool", e))
        block.sync(lambda e: run("sp", e))


class _Stop(Exception):
    pass


_LAST_DUMP_ORDER = []


def build_program(n_sb=NSB, n_layers=2, stop_after=None, dumps=None):
    DEBUG = dumps is not None
    del _LAST_DUMP_ORDER[:]
    nc = bass.Bass("TRN2", target_bir_lowering=False)
    dt_in = lambda name, shape: nc.dram_tensor(name, list(shape), F32, kind="ExternalInput").ap()
    xs = dt_in("xs", [NBLK * 128, D])
    w_in = dt_in("w_in", [2, D, 4608])
    w_bra = dt_in("w_br_att", [2, 512, D])
    w_brs = dt_in("w_br_sgu", [2, 512, D])
    w_out = dt_in("w_out", [2, D, D])
    w_fi = dt_in("w_ffn_in", [2, D, 2 * DFF])
    w_fo = dt_in("w_ffn_out", [2, DFF, D])
    gcols_d = dt_in("gcols", [128, 5, 8])
    lng_d = dt_in("lng", [128, 2, 512])
    lnb_d = dt_in("lnb", [128, 2, 512])
    bsb_d = dt_in("bsb", [128, 2, 512])
    wsT_d = dt_in("wsT", [128, 2, 4, 128])
    maskT_d = dt_in("maskT", [128, 128])
    biasg_d = dt_in("biasg", [2, 128, 8, 640])
    maskb_d = dt_in("maskb", [128, 640])
    bgate_d = dt_in("bgate", [128, 2, 2, 8])
    vones_d = dt_in("vones", [128, NBLK, 128])
    ident_d = dt_in("ident", [128, 128])
    y = nc.dram_tensor("y", [16 * 128, D], F32, kind="ExternalOutput").ap()

    dbg = nc.dram_tensor("dbg", [32, 128, D], F32, kind="ExternalOutput").ap() if DEBUG else None
    es = contextlib.ExitStack()
    with es:
        def sb(name, shape, dt):
            return es.enter_context(nc.sbuf_tensor("sb_" + name, list(shape), dt))

        h = sb("h", [128, 4, D], F32)
        Kr = [sb("Kr%d" % l, [128, 4, 1024], BF16) for l in range(2)]
        Vr = [sb("Vr%d" % l, [128, 8, 512], BF16) for l in range(2)]
        xnT = sb("xnT", [128, 8, 512], BF16)
        R = sb("R", [128, 16384], BF16)
        Wr = sb("Wr", [128, NS, 4096], BF16)
        biasb = sb("biasb", [128, 2, 8, 640], BF16)
        lng = sb("lng", [128, 2, 512], F32)
        lnb = sb("lnb", [128, 2, 512], F32)
        bsb = sb("bsb", [128, 2, 512], F32)
        wsTb = sb("wsTb", [128, 2, 4, 128], BF16)
        vonesb = sb("vonesb", [128, NBLK, 128], BF16)
        identb = sb("identb", [128, 128], BF16)
        gcols = sb("gcols", [128, 5, 8], F32)
        bgate = sb("bgate", [128, 2, 2, 8], F32)
        epsc = sb("epsc", [128, 1], F32)
        tinyc = sb("tinyc", [128, 1], F32)
        xn_tm = sb("xn_tm", [128, 2, D], BF16)
        sc = sb("sc", [128, 2, 640], F32)
        PT = sb("PT", [128, 3, 640], BF16)
        gv = sb("gv", [128, 4, 512], F32)
        gates = sb("gates", [128, 4, 512], F32)
        dtmp = sb("dtmp", [128, 512], F32)
        ss = sb("ss", [128, 4], F32)
        rstd = sb("rstd", [128, 4], F32)
        stats = sb("stats", [128, 4, 6], F32)
        mv = sb("mv", [128, 4, 2], F32)
        lrs = sb("lrs", [128, 4], F32)
        ps = es.enter_context(nc.psum_tensor("ps", [128, 8, 512], F32))
        psflat = ps[:].rearrange("p a b -> p (a b)")
        maskb = sc[:, 0, :]
        maskT = sc[:, 1, 0:128]
        junk = sc[:].rearrange("p a b -> p (a b)")[:, 0:1024]
        JUNK = [("sc", 0), ("sc", 1)]
        sg = gv[:, 0:2, :]
        gfin = gv[:, 2:4, :].rearrange("p a b -> p (a b)")

        Qz = R[:, 0:4096].rearrange("p (c t) -> p c t", c=8)
        uT = R[:, 4096:6144].rearrange("p (c t) -> p c t", c=4)
        vln = R[:, 6144:8192].rearrange("p (c t) -> p c t", c=4)
        attT = R[:, 8192:10240].rearrange("p (c t) -> p c t", c=4)
        sguT = R[:, 10240:12288].rearrange("p (c t) -> p c t", c=4)
        mrgT = R[:, 12288:16384].rearrange("p (c t) -> p c t", c=8)
        actT = R[:, 0:NFC * 512].rearrange("p (c t) -> p c t", c=NFC)
        RQ, RU, RV, RA, RS_, RM = "R_q", "R_u", "R_v", "R_a", "R_s", "R_m"
        R_ALL = [RQ, RU, RV, RA, RS_, RM]

        eng_sems = {e: es.enter_context(nc.semaphore("s_" + e)) for e in ENGS}
        dma_keys = ["c0", "c1", "c2"] + ["w%d" % i for i in range(NS)] + ["x%d" % i for i in range(4)] + \
                   ["o%d" % i for i in range(4)] + ["dbg"]
        dma_sems = {k: es.enter_context(nc.semaphore("d_" + k)) for k in dma_keys}
        block = es.enter_context(nc.Block())
        S = Sched()

        def pk(b):
            return [("ps", b)]

        S.op("sp", lambda e: e.dma_start(out=gcols[:], in_=gcols_d), writes=["gcols"], dma_sem="c0")
        S.op("sp", lambda e: e.dma_start(out=lng[:], in_=lng_d), writes=["lng"], dma_sem="c0")
        S.op("sp", lambda e: e.dma_start(out=lnb[:], in_=lnb_d), writes=["lnb"], dma_sem="c0")
        S.op("sp", lambda e: e.dma_start(out=bsb[:], in_=bsb_d), writes=["bsb"], dma_sem="c0")
        S.op("sp", lambda e: e.dma_start(out=bgate[:], in_=bgate_d), writes=["bgate"], dma_sem="c0")
        S.op("sp", lambda e: e.dma_start(out=maskT, in_=maskT_d), writes=[("sc", 1)], dma_sem="c0")
        S.op("sp", lambda e: e.dma_start(out=maskb, in_=maskb_d), writes=[("sc", 0)], dma_sem="c0")
        wsT_stage = gates[:].rearrange("p a b -> p (a b)")[:, 0:1024].rearrange("p (l g t) -> p l g t", l=2, g=4)
        S.op("sp", lambda e: e.dma_start(out=wsT_stage, in_=wsT_d), writes=[("gates", 0), ("gates", 1)], dma_sem="c0")
        S.dma_group_finalize("c0")
        S.op("pool", lambda e: e.dma_start(out=identb[:], in_=ident_d), writes=["identb"], dma_sem="c1")
        S.op("pool", lambda e: e.dma_start(out=vonesb[:], in_=vones_d), writes=["vonesb"], dma_sem="c1")
        S.dma_group_finalize("c1")
        S.op("dve", lambda e: e.memset(epsc[:], EPS), writes=["epsc"])
        S.op("dve", lambda e: e.memset(tinyc[:], 1e-20), writes=["tinyc"])
        S.op("dve", lambda e: e.tensor_tensor(out=wsTb[:].rearrange("p l g t -> p (l g) t"),
                                              in0=wsT_stage.rearrange("p l g t -> p (l g) t"),
                                              in1=maskT.unsqueeze(1).to_broadcast([128, 8, 128]), op=ALU.mult),
             reads=[("gates", 0), ("gates", 1), ("sc", 1)], writes=["wsTb"])
        Rf = R[:].bitcast(F32)
        bstage = Rf[:, 0:5120].rearrange("p (h c) -> p h c", h=8)
        for l in range(2):
            S.op("sp", (lambda l: lambda e: e.dma_start(out=bstage, in_=biasg_d[l]))(l), writes=R_ALL, dma_sem="c2")
            S.op("dve", (lambda l: lambda e: e.tensor_tensor(out=biasb[:, l, :, :], in0=bstage,
                                                             in1=maskb.unsqueeze(1).to_broadcast([128, 8, 640]),
                                                             op=ALU.add))(l),
                 reads=R_ALL + [("sc", 0)], writes=["biasb"])

        def units_for(l, kvonly):
            u = []

            def win(c0, n=512):
                return w_in[l][:, c0:c0 + n].rearrange("(c p) f -> p c f", p=128)

            def slot_view(kc, n):
                return (kc, n)

            if kvonly:
                u.append(("k", [((8, 512), win(512))]))
                u.append(("v", [((8, 512), win(1024))]))
                return u
            u.append(("vs", [((8, 512), win(2048))]))
            u.append(("u", [((8, 512), win(1536))]))
            u.append(("q", [((8, 512), win(0))]))
            u.append(("k", [((8, 512), win(512))]))
            u.append(("v", [((8, 512), win(1024))]))
            for hf in range(2):
                u.append(("br%d" % hf, [((4, 512, 0), w_bra[l][:, hf * 512:(hf + 1) * 512].rearrange("(c p) f -> p c f", p=128)),
                                        ((4, 512, 4), w_brs[l][:, hf * 512:(hf + 1) * 512].rearrange("(c p) f -> p c f", p=128))]))
                u.append(("g0%d" % hf, [((8, 512), win(2560 + hf * 512))]))
                u.append(("g1%d" % hf, [((8, 512), win(3584 + hf * 512))]))
            for hf in range(2):
                u.append(("wo%d" % hf, [((8, 512), w_out[l][:, hf * 512:(hf + 1) * 512].rearrange("(c p) f -> p c f", p=128))]))
            for i in range(6):
                n = 512 if i < 5 else 256
                u.append(("fg%d" % i, [((8, n), w_fi[l][:, i * 512:i * 512 + n].rearrange("(c p) f -> p c f", p=128))]))
                u.append(("fu%d" % i, [((8, n), w_fi[l][:, DFF + i * 512:DFF + i * 512 + n].rearrange("(c p) f -> p c f", p=128))]))
            for hf in range(2):
                for (c0, ncx) in ((0, 8), (8, 8), (16, 6)):
                    u.append(("fo%d_%d" % (hf, c0), [((ncx, 512), w_fo[l][c0 * 128:(c0 + ncx) * 128, hf * 512:(hf + 1) * 512]
                                                       .rearrange("(c p) f -> p c f", p=128))]))
            return u

        passes = []
        for s in range(n_sb):
            if s == 0:
                passes.append((0, s, True))
            else:
                passes.append((0, s, False))
                if n_layers > 1:
                    passes.append((1, s, s == 1))
        all_units = []
        for (l, s, kvonly) in passes:
            for (kind, dmas) in units_for(l, kvonly):
                all_units.append((l, s, kind, dmas))
        wstate = {"next_issue": 0, "next_use": 0}

        def issue_unit(n):
            l, s, kind, dmas = all_units[n]
            slot = n % NS
            for (spec, src) in dmas:
                if len(spec) == 2:
                    kc, ncol = spec
                    k0 = 0
                else:
                    kc, ncol, k0 = spec
                dst = Wr[:, slot, :].rearrange("p (c f) -> p c f", c=8)[:, k0:k0 + kc, 0:ncol] if ncol == 512 else \
                    Wr[:, slot, 0:8 * ncol].rearrange("p (c f) -> p c f", c=8)[:, k0:k0 + kc, :]
                S.op("pool", (lambda dst, src: lambda e: e.dma_start(out=dst, in_=src))(dst, src),
                     writes=[("w", slot, 0), ("w", slot, 1)], dma_sem="w%d" % slot)
            S.dma_group_finalize("w%d" % slot)

        def w_prefetch():
            while wstate["next_issue"] < len(all_units) and wstate["next_issue"] < wstate["next_use"] + NS:
                issue_unit(wstate["next_issue"])
                wstate["next_issue"] += 1

        def w_acquire(l, s, kind):
            n = wstate["next_use"]
            ul, us, ukind, dmas = all_units[n]
            assert (ul, us, ukind) == (l, s, kind), ((ul, us, ukind), (l, s, kind))
            assert wstate["next_issue"] > n
            wstate["next_use"] += 1
            slot = n % NS
            ncol = dmas[0][0][1]
            view = Wr[:, slot, 0:8 * ncol].rearrange("p (c f) -> p c f", c=8)
            return view, [("w", slot, 0), ("w", slot, 1)]

        def w_release():
            w_prefetch()

        w_prefetch()

        def blk_slot(b):
            return b % 8

        def load_x(s):
            for j in range(4):
                b = 4 * s + j
                S.op("sp", (lambda b, j: lambda e: e.dma_start(out=h[:, j, :], in_=xs[b * 128:(b + 1) * 128, :]))(b, j),
                     writes=[("h", j)], dma_sem="x%d" % j)

        tp_bank = [6]

        def norm_stats(j):
            S.op("act", lambda e: e.activation(out=junk, in_=h[:, j, :], func=AF.Square, accum_out=ss[:, j:j + 1]),
                 reads=[("h", j)], writes=JUNK + [("ss", j)])
            S.op("act", lambda e: e.activation(out=rstd[:, j:j + 1], in_=ss[:, j:j + 1], func=AF.Sqrt, scale=1.0 / D, bias=epsc[:]),
                 reads=[("ss", j), "epsc"], writes=[("rstd", j)])
            S.op("dve", lambda e: e.reciprocal(out=rstd[:, j:j + 1], in_=rstd[:, j:j + 1]), reads=[("rstd", j)], writes=[("rstd", j)])

        def stage_norm(gi):
            for j in range(4):
                norm_stats(j)
                xb = j % 2
                S.op("pool", (lambda j, xb: lambda e: e.tensor_scalar(out=xn_tm[:, xb, :], in0=h[:, j, :], scalar1=rstd[:, j:j + 1],
                                                                       scalar2=None, op0=ALU.mult))(j, xb),
                     reads=[("h", j), ("rstd", j)], writes=[("xn_tm", xb)])
                bank = tp_bank[0]
                tp_bank[0] = 6 if bank == 7 else 7
                pT = ps[:, bank, :].bitcast(BF16).rearrange("p (c t) -> p c t", c=8)
                for c in range(8):
                    S.op("pe", (lambda c, xb, pT: lambda e: e.transpose(out=pT[:, c, :], in_=xn_tm[:, xb, c * 128:(c + 1) * 128],
                                                                         identity=identb[:]))(c, xb, pT),
                         reads=[("xn_tm", xb), "identb"], writes=pk(bank))
                S.op("dve", (lambda j, pT: lambda e: e.tensor_tensor(
                    out=xnT[:, :, j * 128:(j + 1) * 128], in0=pT,
                    in1=gcols[:, gi, :].unsqueeze(2).to_broadcast([128, 8, 128]), op=ALU.mult))(j, pT),
                    reads=pk(bank) + ["gcols"], writes=[("xnT", j)])

        XNT_ALL = [("xnT", j) for j in range(4)]
        acc_bank = [0]

        def next_bank(lo=0, n=4):
            b = lo + acc_bank[0] % n
            acc_bank[0] += 1
            return b

        def proj_fm(l, s, kind, evac):
            wv, wkey = w_acquire(l, s, kind)
            for fc in range(4):
                bank = next_bank()
                for k in range(8):
                    S.op("pe", (lambda fc, k, bank: lambda e: e.matmul(ps[:, bank, :], lhsT=wv[:, k, fc * 128:(fc + 1) * 128],
                                                                       rhs=xnT[:, k, :], start=(k == 0), stop=(k == 7)))(fc, k, bank),
                         reads=wkey + XNT_ALL, writes=pk(bank))
                evac(fc, bank)
            w_release()

        def proj_tm(l, s, kind, evac):
            wv, wkey = w_acquire(l, s, kind)
            for j in range(4):
                bank = next_bank()
                for k in range(8):
                    S.op("pe", (lambda j, k, bank: lambda e: e.matmul(ps[:, bank, :], lhsT=xnT[:, k, j * 128:(j + 1) * 128],
                                                                      rhs=wv[:, k, :], start=(k == 0), stop=(k == 7)))(j, k, bank),
                         reads=wkey + [("xnT", j)], writes=pk(bank))
                evac(j, bank)
            w_release()

        def stage_proj(l, s, kvonly):
            koff = ((4 * s) % 8) * 128

            def ev_q(fc, bank):
                if fc == 0:
                    S.op("pool", lambda e: e.memset(Qz[64:128, 0:8:2, :], 0.0), writes=[RQ])
                    S.op("pool", lambda e: e.memset(Qz[0:64, 1:8:2, :], 0.0), writes=[RQ])
                S.op("act", lambda e: e.activation(out=Qz[0:64, 2 * fc, :], in_=ps[0:64, bank, :], func=AF.Copy, scale=0.125),
                     reads=pk(bank), writes=[RQ, ("Qh", 2 * fc)])
                S.op("act", lambda e: e.activation(out=Qz[64:128, 2 * fc + 1, :], in_=ps[64:128, bank, :], func=AF.Copy, scale=0.125),
                     reads=pk(bank), writes=[RQ, ("Qh", 2 * fc + 1)])

            def ev_k(fc, bank):
                S.op("dve", lambda e: e.tensor_copy(out=Kr[l][:, fc, koff:koff + 512], in_=ps[:, bank, :]),
                     reads=pk(bank), writes=[("Kr", l, (4 * s) % 8 // 4)])

            def ev_v(j, bank):
                slot = blk_slot(4 * s + j)
                S.op("dve", lambda e: e.tensor_copy(out=Vr[l][:, slot, :], in_=ps[:, bank, :]),
                     reads=pk(bank), writes=[("Vr", l, slot)])

            def ev_u(fc, bank):
                S.op("act", lambda e: e.activation(out=uT[:, fc, :], in_=ps[:, bank, :], func=AF.Gelu_apprx_tanh),
                     reads=pk(bank), writes=[RU])

            def ev_vs(j, bank):
                S.op("act", lambda e: e.activation(out=gv[:, j, :], in_=ps[:, bank, :], func=AF.Gelu_apprx_tanh),
                     reads=pk(bank), writes=[("gv", j)])
                S.op("dve", lambda e: e.bn_stats(out=stats[:, j, :], in_=gv[:, j, :]), reads=[("gv", j)], writes=[("stats", j)])
                S.op("dve", lambda e: e.bn_aggr(out=mv[:, j, :], in_=stats[:, j, :]), reads=[("stats", j)], writes=[("mv", j)])

            if kvonly:
                proj_fm(l, s, "k", ev_k)
                proj_tm(l, s, "v", ev_v)
                return
            proj_tm(l, s, "vs", ev_vs)
            proj_fm(l, s, "u", ev_u)
            S.op("act", lambda e: e.activation(out=lrs[:], in_=mv[:, :, 1], func=AF.Sqrt, scale=1.0, bias=epsc[:]),
                 reads=[("mv", j) for j in range(4)] + ["epsc"], writes=["lrs"])
            S.op("dve", lambda e: e.reciprocal(out=lrs[:], in_=lrs[:]), reads=["lrs"], writes=["lrs"])
            for j in range(4):
                S.op("dve", (lambda j: lambda e: e.tensor_scalar(out=gv[:, j, :], in0=gv[:, j, :], scalar1=mv[:, j, 0:1],
                                                                  scalar2=lrs[:, j:j + 1], op0=ALU.subtract, op1=ALU.mult))(j),
                     reads=[("gv", j), ("mv", j), "lrs"], writes=[("gv", j)])
                S.op("dve", (lambda j: lambda e: e.tensor_tensor(out=gv[:, j, :], in0=gv[:, j, :], in1=lng[:, l, :], op=ALU.mult))(j),
                     reads=[("gv", j), "lng"], writes=[("gv", j)])
                S.op("dve", (lambda j: lambda e: e.tensor_tensor(out=vln[:, j, :], in0=gv[:, j, :], in1=lnb[:, l, :], op=ALU.add))(j),
                     reads=[("gv", j), "lnb"], writes=[RV])
            proj_fm(l, s, "q", ev_q)
            proj_fm(l, s, "k", ev_k)
            proj_tm(l, s, "v", ev_v)

        sbuf_ctr = {"sA": 0, "sc": 0, "pt": 0, "dt": 0}

        def stage_attn(l, s):
            units = [(j, hh) for j in range(4) for hh in range(8)]
            st = {}

            def emit_qk(n):
                j, hh = units[n]
                p = hh // 2
                b = 4 * s + j
                kslots = [blk_slot(b - 4 + i) for i in range(5)]
                kreads = [("Kr", l, ks // 4) for ks in set(kslots)]
                sbi = sbuf_ctr["sA"] % 3
                sbuf_ctr["sA"] += 1
                base = sbi * 1024
                banks = [2 * sbi, 2 * sbi + 1]
                for i in range(5):
                    ks = kslots[i]
                    col = base + i * 128
                    S.op("pe", (lambda col, ks, p, hh, j: lambda e: e.matmul(
                        psflat[:, col:col + 128], lhsT=Kr[l][:, p, ks * 128:(ks + 1) * 128],
                        rhs=Qz[:, hh, j * 128:(j + 1) * 128], start=True, stop=True))(col, ks, p, hh, j),
                        reads=kreads + [RQ, ("Qh", hh)], writes=pk(col // 512))
                sci = sbuf_ctr["sc"] % 2
                sbuf_ctr["sc"] += 1
                S.op("dve", (lambda base, hh, sci: lambda e: e.tensor_tensor(
                    out=sc[:, sci, :], in0=psflat[:, base:base + 640], in1=biasb[:, l, hh, :], op=ALU.add))(base, hh, sci),
                    reads=pk(banks[0]) + pk(banks[1]) + ["biasb"], writes=[("sc", sci)])
                pi = sbuf_ctr["pt"] % 3
                sbuf_ctr["pt"] += 1
                S.op("act", (lambda sci, pi: lambda e: e.activation(out=PT[:, pi, :], in_=sc[:, sci, :], func=AF.Exp))(sci, pi),
                     reads=[("sc", sci)], writes=[("PT", pi)])
                st[n] = (kslots, pi)

            def emit_pv(n):
                j, hh = units[n]
                p, par = hh // 2, hh % 2
                b = 4 * s + j
                kslots, pi = st.pop(n)
                bank = 6 + p % 2
                for kind in ("num", "den"):
                    cc = par * 128 + (0 if kind == "num" else 256)
                    for i in range(5):
                        ks = kslots[i]
                        if kind == "num":
                            lhs = Vr[l][:, ks, p * 128:(p + 1) * 128]
                            rd = [("Vr", l, ks)]
                        else:
                            lhs = vonesb[:, b - 4 + i, :]
                            rd = ["vonesb"]
                        S.op("pe", (lambda lhs, pi, i, bank, cc: lambda e: e.matmul(
                            ps[:, bank, cc:cc + 128], lhsT=lhs,
                            rhs=PT[:, pi, i * 128:(i + 1) * 128], start=(i == 0), stop=(i == 4)))(lhs, pi, i, bank, cc),
                            reads=rd + [("PT", pi)], writes=pk(bank))
                if par != 1:
                    return None

                def norm():
                    di = sbuf_ctr["dt"] % 2
                    sbuf_ctr["dt"] += 1
                    dsl = dtmp[:, di * 256:(di + 1) * 256]
                    S.op("act", (lambda bank, dsl: lambda e: e.activation(out=dsl, in_=ps[:, bank, 256:512], func=AF.Ln,
                                                                          bias=tinyc[:]))(bank, dsl),
                         reads=pk(bank) + ["tinyc"], writes=[("dtmp", di)])
                    S.op("act", (lambda dsl: lambda e: e.activation(out=dsl, in_=dsl, func=AF.Exp, scale=-1.0))(dsl),
                         reads=[("dtmp", di)], writes=[("dtmp", di)])
                    for hf2 in range(2):
                        rs = slice(hf2 * 64, (hf2 + 1) * 64)
                        S.op("dve", (lambda j, p, bank, dsl, rs, hf2: lambda e: e.tensor_tensor(
                            out=attT[rs, p, j * 128:(j + 1) * 128], in0=ps[rs, bank, hf2 * 128:(hf2 + 1) * 128],
                            in1=dsl[rs, hf2 * 128:(hf2 + 1) * 128], op=ALU.mult))(j, p, bank, dsl, rs, hf2),
                            reads=pk(bank) + [("dtmp", di)], writes=[RA])
                return norm

            nu = len(units)
            pending = []
            for n in range(min(ATT_SKEW, nu)):
                emit_qk(n)
            for n in range(nu):
                if n + ATT_SKEW < nu:
                    emit_qk(n + ATT_SKEW)
                f = emit_pv(n)
                if f is not None:
                    pending.append(f)
                if len(pending) > 1:
                    pending.pop(0)()
            while pending:
                pending.pop(0)()

        def stage_sgu(l, s):
            for j in range(4):
                bank = next_bank()
                for g in range(4):
                    S.op("pe", (lambda j, g, bank: lambda e: e.matmul(ps[:, bank, g * 128:(g + 1) * 128],
                                                                      lhsT=vln[:, j, g * 128:(g + 1) * 128],
                                                                      rhs=wsTb[:, l, g, :], start=True, stop=True))(j, g, bank),
                         reads=[RV, "wsTb"], writes=pk(bank))
                gb = gates[:, j % 2, :]
                S.op("dve", (lambda bank, gb: lambda e: e.tensor_tensor(out=gb, in0=ps[:, bank, :], in1=bsb[:, l, :], op=ALU.add))(bank, gb),
                     reads=pk(bank) + ["bsb"], writes=[("gates", j % 2)])
                S.op("dve", (lambda j, gb: lambda e: e.tensor_tensor(
                    out=sguT[:, :, j * 128:(j + 1) * 128], in0=gb.rearrange("p (c t) -> p c t", c=4),
                    in1=uT[:, :, j * 128:(j + 1) * 128], op=ALU.mult))(j, gb),
                    reads=[("gates", j % 2), RU], writes=[RS_])

        def stage_merge(l, s):
            for hf in range(2):
                wbr, kbr = w_acquire(l, s, "br%d" % hf)
                wg0, kg0 = w_acquire(l, s, "g0%d" % hf)
                wg1, kg1 = w_acquire(l, s, "g1%d" % hf)
                for f4 in range(4):
                    fc = hf * 4 + f4
                    bset = (fc % 2) * 4
                    bA, bB, bG0, bG1 = bset, bset + 1, bset + 2, bset + 3
                    cs = slice(f4 * 128, (f4 + 1) * 128)
                    for k in range(4):
                        S.op("pe", (lambda k, bA, cs, wbr: lambda e: e.matmul(ps[:, bA, :], lhsT=wbr[:, k, cs], rhs=attT[:, k, :],
                                                                              start=(k == 0), stop=(k == 3)))(k, bA, cs, wbr),
                             reads=kbr + [RA], writes=pk(bA))
                    for k in range(4):
                        S.op("pe", (lambda k, bB, cs, wbr: lambda e: e.matmul(ps[:, bB, :], lhsT=wbr[:, 4 + k, cs], rhs=sguT[:, k, :],
                                                                              start=(k == 0), stop=(k == 3)))(k, bB, cs, wbr),
                             reads=kbr + [RS_], writes=pk(bB))
                    for (wg, kg, bG, gi) in ((wg0, kg0, bG0, 0), (wg1, kg1, bG1, 1)):
                        for k in range(8):
                            S.op("pe", (lambda k, bG, cs, wg: lambda e: e.matmul(ps[:, bG, :], lhsT=wg[:, k, cs], rhs=xnT[:, k, :],
                                                                                 start=(k == 0), stop=(k == 7)))(k, bG, cs, wg),
                                 reads=kg + XNT_ALL, writes=pk(bG))
                        gidx = (fc % 2) * 2 + gi
                        S.op("act", (lambda bG, gidx, gi, fc: lambda e: e.activation(
                            out=gates[:, gidx, :], in_=ps[:, bG, :], func=AF.Sigmoid, bias=bgate[:, l, gi, fc:fc + 1]))(bG, gidx, gi, fc),
                            reads=pk(bG) + ["bgate"], writes=[("gates", gidx)])
                    g0i, g1i = (fc % 2) * 2, (fc % 2) * 2 + 1
                    S.op("dve", (lambda bA, g0i: lambda e: e.tensor_tensor(out=gates[:, g0i, :], in0=ps[:, bA, :], in1=gates[:, g0i, :],
                                                                           op=ALU.mult))(bA, g0i),
                         reads=pk(bA) + [("gates", g0i)], writes=[("gates", g0i)])
                    S.op("dve", (lambda bB, g1i: lambda e: e.tensor_tensor(out=gates[:, g1i, :], in0=ps[:, bB, :], in1=gates[:, g1i, :],
                                                                           op=ALU.mult))(bB, g1i),
                         reads=pk(bB) + [("gates", g1i)], writes=[("gates", g1i)])
                    S.op("dve", (lambda fc, g0i, g1i: lambda e: e.tensor_tensor(out=mrgT[:, fc, :], in0=gates[:, g0i, :],
                                                                                in1=gates[:, g1i, :], op=ALU.add))(fc, g0i, g1i),
                         reads=[("gates", g0i), ("gates", g1i)], writes=[RM])
                w_release()
                w_release()
                w_release()

        def stage_wout(l, s):
            for hf in range(2):
                wv, wkey = w_acquire(l, s, "wo%d" % hf)
                for j in range(4):
                    bank = next_bank()
                    for k in range(8):
                        S.op("pe", (lambda j, k, bank, wv: lambda e: e.matmul(ps[:, bank, :], lhsT=mrgT[:, k, j * 128:(j + 1) * 128],
                                                                              rhs=wv[:, k, :], start=(k == 0), stop=(k == 7)))(j, k, bank, wv),
                             reads=wkey + [RM], writes=pk(bank))
                    S.op("dve", (lambda j, bank, hf: lambda e: e.tensor_tensor(out=h[:, j, hf * 512:(hf + 1) * 512], in0=ps[:, bank, :],
                                                                               in1=h[:, j, hf * 512:(hf + 1) * 512], op=ALU.add))(j, bank, hf),
                         reads=pk(bank) + [("h", j)], writes=[("h", j)])
                w_release()

        def stage_ffn_in(l, s):
            for i in range(6):
                wg, kg = w_acquire(l, s, "fg%d" % i)
                wu, ku = w_acquire(l, s, "fu%d" % i)
                nch = 4 if i < 5 else 2
                for c4 in range(nch):
                    c = i * 4 + c4
                    bset = (c % 4) * 2
                    bG, bU = bset, bset + 1
                    cs = slice(c4 * 128, (c4 + 1) * 128)
                    for (wv, wk, bank) in ((wg, kg, bG), (wu, ku, bU)):
                        for k in range(8):
                            S.op("pe", (lambda k, bank, cs, wv: lambda e: e.matmul(ps[:, bank, :], lhsT=wv[:, k, cs], rhs=xnT[:, k, :],
                                                                                   start=(k == 0), stop=(k == 7)))(k, bank, cs, wv),
                                 reads=wk + XNT_ALL, writes=pk(bank))
                    si = c % 2
                    S.op("act", (lambda bG, si: lambda e: e.activation(out=sg[:, si, :], in_=ps[:, bG, :], func=AF.Silu))(bG, si),
                         reads=pk(bG), writes=[("gv", si)])
                    S.op("dve", (lambda bU, si, c: lambda e: e.tensor_tensor(out=actT[:, c, :], in0=ps[:, bU, :], in1=sg[:, si, :],
                                                                             op=ALU.mult))(bU, si, c),
                         reads=pk(bU) + [("gv", si)], writes=R_ALL)
                w_release()
                w_release()

        def stage_ffn_out(l, s):
            for hf in range(2):
                ws = []
                for (c0, ncx) in ((0, 8), (8, 8), (16, 6)):
                    wv, wkey = w_acquire(l, s, "fo%d_%d" % (hf, c0))
                    ws.append((c0, ncx, wv, wkey))
                for j in range(4):
                    bank = next_bank()
                    for (c0, ncx, wv, wkey) in ws:
                        for cc in range(ncx):
                            c = c0 + cc
                            S.op("pe", (lambda j, c, cc, bank, wv: lambda e: e.matmul(
                                ps[:, bank, :], lhsT=actT[:, c, j * 128:(j + 1) * 128], rhs=wv[:, cc, :],
                                start=(c == 0), stop=(c == NFC - 1)))(j, c, cc, bank, wv),
                                reads=wkey + R_ALL, writes=pk(bank))
                    S.op("dve", (lambda j, bank, hf: lambda e: e.tensor_tensor(out=h[:, j, hf * 512:(hf + 1) * 512], in0=ps[:, bank, :],
                                                                               in1=h[:, j, hf * 512:(hf + 1) * 512], op=ALU.add))(j, bank, hf),
                         reads=pk(bank) + [("h", j)], writes=[("h", j)])
                w_release()
                w_release()
                w_release()

        def stage_final(s):
            S.op("sp", lambda e: e.dma_start(out=gfin, in_=gfin_d), writes=[("gv", 2), ("gv", 3)], dma_sem="c0")
            S.dma_group_finalize("c0")
            yst = R[:].bitcast(F32)[:, 0:4 * D].rearrange("p (j f) -> p j f", j=4)
            for j in range(4):
                norm_stats(j)
                S.op("dve", (lambda j: lambda e: e.scalar_tensor_tensor(
                    out=yst[:, j, :], in0=h[:, j, :], scalar=rstd[:, j:j + 1], in1=gfin, op0=ALU.mult, op1=ALU.mult))(j),
                    reads=[("h", j), ("rstd", j), ("gv", 2), ("gv", 3)], writes=(R_ALL if j == 0 else []) + [("yst", j)])
                ob = (4 * s + j) - 8
                S.op("sp", (lambda j, ob: lambda e: e.dma_start(out=y[ob * 128:(ob + 1) * 128, :], in_=yst[:, j, :]))(j, ob),
                     reads=R_ALL + [("yst", j)], dma_sem="o%d" % j)

        gfin_d = dt_in("gfin", [128, D])

        dstage = sb("dstage", [128, D], F32) if DEBUG else None
        dctr = [0]

        def dump(name, ap, reads, n=D):
            if not DEBUG or name not in dumps:
                return
            idx = dctr[0]
            dctr[0] += 1
            _LAST_DUMP_ORDER.append(name)
            S.op("dve", lambda e: e.tensor_copy(out=dstage[:, 0:n], in_=ap), reads=reads, writes=["dstage"])
            S.op("sp", lambda e: e.dma_start(out=dbg[idx][:, 0:n], in_=dstage[:, 0:n]), reads=["dstage"], dma_sem="dbg")

        def chk(tag):
            if stop_after == tag:
                raise _Stop()

        def run_layer(l, s, kvonly):
            T = "L%d_S%d_" % (l, s)
            stage_norm(2 * l)
            dump(T + "xnT0", xnT[:, 0, :], XNT_ALL, 512)
            dump(T + "xnT7", xnT[:, 7, :], XNT_ALL, 512)
            chk(T + "norm")
            stage_proj(l, s, kvonly)
            dump(T + "K0", Kr[l][:, 0, :], [("Kr", l, 0), ("Kr", l, 1)], 1024)
            dump(T + "V0", Vr[l][:, blk_slot(4 * s), :], [("Vr", l, blk_slot(4 * s))], 512)
            if kvonly:
                chk(T + "proj")
                return
            dump(T + "q0", Qz[:, 0, :], [RQ, ("Qh", 0)], 512)
            dump(T + "u0", uT[:, 0, :], [RU], 512)
            dump(T + "vln0", vln[:, 0, :], [RV], 512)
            chk(T + "proj")
            stage_attn(l, s)
            dump(T + "att0", attT[:, 0, :], [RA], 512)
            dump(T + "att3", attT[:, 3, :], [RA], 512)
            chk(T + "attn")
            stage_sgu(l, s)
            dump(T + "sgu0", sguT[:, 0, :], [RS_], 512)
            chk(T + "sgu")
            stage_merge(l, s)
            for fcx in range(8):
                dump(T + "mrg%d" % fcx, mrgT[:, fcx, :], [RM], 512)
            chk(T + "merge")
            stage_wout(l, s)
            dump(T + "h0mid", h[:, 0, :], [("h", 0)])
            chk(T + "wout")
            stage_norm(2 * l + 1)
            stage_ffn_in(l, s)
            dump(T + "act0", actT[:, 0, :], R_ALL, 512)
            dump(T + "act21", actT[:, 21, :], R_ALL, 512)
            chk(T + "ffn_in")
            stage_ffn_out(l, s)
            dump(T + "h0", h[:, 0, :], [("h", 0)])
            dump(T + "h3", h[:, 3, :], [("h", 3)])
            chk(T + "ffn_out")

        try:
            for s in range(n_sb):
                load_x(s)
                run_layer(0, s, s == 0)
                if s >= 1 and n_layers > 1:
                    run_layer(1, s, s == 1)
                if s >= 2:
                    stage_final(s)
            assert wstate["next_use"] == len(all_units), (wstate, len(all_units))
        except _Stop:
            pass

        S.final_waits["sp"] = [("o%d" % j, S.dma_counts.get("o%d" % j, 0)) for j in range(4)]
        S.final_waits["pool"] = [(k, v) for k, v in S.dma_counts.items() if k.startswith("w") or k == "c1"]
        S.final_waits["sp"] += [(k, v) for k, v in S.dma_counts.items() if k.startswith("x") or k in ("c0", "c2")]
        if DEBUG:
            S.final_waits["sp"].append(("dbg", S.dma_counts.get("dbg", 0)))
        S.emit(block, eng_sems, dma_sems)
    return nc


def _host_consts(att_rel_bias, sgu_norm_gain, sgu_norm_bias, sgu_w, sgu_b, b_gate, norm_mix, norm_ffn, norm_final):
    f32 = np.float32
    gl = [norm_mix[0], norm_ffn[0], norm_mix[1], norm_ffn[1], norm_final]
    gcols = np.stack([np.asarray(g, f32).reshape(8, 128).T for g in gl], axis=1)
    lng = np.broadcast_to(np.asarray(sgu_norm_gain, f32)[None], (128, 2, 512)).copy()
    lnb = np.broadcast_to(np.asarray(sgu_norm_bias, f32)[None], (128, 2, 512)).copy()
    bsb = np.broadcast_to(np.asarray(sgu_b, f32).reshape(2, 512)[None], (128, 2, 512)).copy()
    wsT = np.ascontiguousarray(np.asarray(sgu_w, f32).transpose(3, 0, 1, 2))
    maskT = (np.arange(128)[:, None] <= np.arange(128)[None, :]).astype(f32)
    ki = np.arange(128)[:, None, None]
    i5 = np.arange(5)[None, :, None]
    qi = np.arange(128)[None, None, :]
    rel = np.clip((4 - i5) * 128 + qi - ki, -128, 128) + 128
    biasg = np.asarray(att_rel_bias, f32)[:, :, rel]
    biasg = np.ascontiguousarray(biasg.transpose(0, 2, 1, 3, 4)).reshape(2, 128, 8, 640)
    dchunk = (8 - 2 * i5) + (qi // 64) - (ki // 64)
    valid = (dchunk >= 0) & (dchunk <= 8)
    maskb = np.where(valid, 0.0, NEG).astype(f32).reshape(128, 640)
    bgate = np.ascontiguousarray(np.asarray(b_gate, f32).reshape(2, 2, 8, 128).transpose(3, 0, 1, 2))
    gfin = np.broadcast_to(np.asarray(norm_final, f32)[None], (128, D)).copy()
    ident = np.eye(128, dtype=f32)
    return dict(gcols=gcols, lng=lng, lnb=lnb, bsb=bsb, wsT=wsT, maskT=maskT, biasg=biasg, maskb=maskb,
                bgate=bgate, gfin=gfin, ident=ident)


def _make_in_maps(x, shared):
    f32 = np.float32
    in_maps = []
    for c in range(8):
        b, half = c // 2, c % 2
        xs = np.zeros((NBLK * 128, D), f32)
        vones = np.ones((128, NBLK, 128), f32)
        if half == 0:
            xs[1024:] = x[b, 0:2048]
            vones[:, 0:8, :] = 0.0
        else:
            xs[:] = x[b, 1024:4096]
        m = dict(shared)
        m["xs"] = xs
        m["vones"] = vones
        in_maps.append(m)
    return in_maps


_PROGRAM = {}


def kernel(x, norm_mix, w_in, att_rel_bias, sgu_norm_gain, sgu_norm_bias, sgu_w, sgu_b,
           w_br_att, w_br_sgu, b_gate, w_out, norm_ffn, w_ffn_in, w_ffn_out, norm_final):
    f32 = np.float32
    x = np.asarray(x, f32)
    consts = _host_consts(att_rel_bias, sgu_norm_gain, sgu_norm_bias, sgu_w, sgu_b, b_gate, norm_mix, norm_ffn, norm_final)
    shared = dict(w_in=np.ascontiguousarray(w_in, f32), w_br_att=np.ascontiguousarray(w_br_att, f32),
                  w_br_sgu=np.ascontiguousarray(w_br_sgu, f32), w_out=np.ascontiguousarray(w_out, f32),
                  w_ffn_in=np.ascontiguousarray(w_ffn_in, f32), w_ffn_out=np.ascontiguousarray(w_ffn_out, f32))
    shared.update(consts)
    in_maps = _make_in_maps(x, shared)
    if "nc" not in _PROGRAM:
        _PROGRAM["nc"] = build_program()
    res = run_bass_kernel_spmd(_PROGRAM["nc"], in_maps, core_ids=list(range(8)))
    out = np.empty((4, 4096, D), f32)
    for c in range(8):
        b, half = c // 2, c % 2
        out[b, half * 2048:(half + 1) * 2048] = res.results[c]["y"]
    return out
```

```python
import contextlib
import numpy as np
import concourse.bass as bass
import concourse.mybir as mybir
from concourse.bass_utils import run_bass_kernel_spmd

F32 = mybir.dt.float32
BF16 = mybir.dt.bfloat16
AF = mybir.ActivationFunctionType
ALU = mybir.AluOpType

D = 1024
DFF = 2816
NFC = 22
NBLK = 24
NSB = 6
NS = 6
EPS = 1e-6
NEG = -30000.0
ATT_SKEW = 2

ENGS = ("pe", "act", "dve", "pool", "sp")


class Op:
    __slots__ = ("eng", "idx", "fn", "waits", "signal", "ev_sem", "ev_val", "is_dma", "sigidx")

    def __init__(self, eng, idx, fn, is_dma):
        self.eng = eng
        self.idx = idx
        self.fn = fn
        self.waits = []
        self.signal = False
        self.is_dma = is_dma
        self.ev_sem = None
        self.ev_val = None
        self.sigidx = None


class Sched:
    def __init__(self):
        self.ops = {e: [] for e in ENGS}
        self.last_w = {}
        self.readers = {}
        self.dma_counts = {}
        self.waited = {e: {} for e in ENGS}
        self.final_waits = {}
        self.group_ops = {}

    def _add_dep(self, op, dep):
        if dep is None or dep is op:
            return
        if dep.is_dma:
            key = ("dma", dep.ev_sem)
            prev = self.waited[op.eng].get(key, 0)
            if dep.ev_val <= prev:
                return
            self.waited[op.eng][key] = dep.ev_val
            op.waits.append((dep.ev_sem, dep.ev_val))
            return
        if dep.eng == op.eng:
            return
        prev = self.waited[op.eng].get(dep.eng, -1)
        if dep.idx <= prev:
            return
        self.waited[op.eng][dep.eng] = dep.idx
        dep.signal = True
        op.waits.append(dep)

    def op(self, eng, fn, reads=(), writes=(), dma_sem=None):
        is_dma = dma_sem is not None
        excl = [b for b in reads if isinstance(b, tuple) and b[0] == "ps"]
        if excl:
            reads = [b for b in reads if not (isinstance(b, tuple) and b[0] == "ps")]
            writes = list(writes) + excl
        o = Op(eng, len(self.ops[eng]), fn, is_dma)
        if is_dma:
            c = self.dma_counts.get(dma_sem, 0) + 16
            self.dma_counts[dma_sem] = c
            o.ev_sem = dma_sem
            o.ev_val = c
            self.group_ops.setdefault(dma_sem, []).append(o)
        for b in reads:
            w = self.last_w.get(b)
            if w is not None:
                if (not w.is_dma) and w.eng == eng and not is_dma:
                    if eng != "pe" and o.idx - w.idx <= 2:
                        w.signal = True
                        if w not in o.waits:
                            o.waits.append(w)
                else:
                    self._add_dep(o, w)
        for b in writes:
            w = self.last_w.get(b)
            if w is not None:
                self._add_dep(o, w)
            for r in self.readers.get(b, {}).values():
                self._add_dep(o, r)
        for b in reads:
            self.readers.setdefault(b, {})[(eng, o.ev_sem) if is_dma else eng] = o
        for b in writes:
            self.last_w[b] = o
            self.readers[b] = {}
        self.ops[eng].append(o)
        return o

    def dma_group_finalize(self, key):
        tot = self.dma_counts.get(key, 0)
        for o in self.group_ops.get(key, []):
            o.ev_val = tot
        self.group_ops[key] = []

    def emit(self, block, sems, dma_sems):
        for e in ENGS:
            n = 0
            for o in self.ops[e]:
                if (not o.is_dma) and o.signal:
                    n += 1
                    o.sigidx = n

        def run(engname, engine):
            for o in self.ops[engname]:
                for d in o.waits:
                    if isinstance(d, tuple):
                        engine.wait_ge(dma_sems[d[0]], d[1])
                    else:
                        engine.wait_ge(sems[d.eng], d.sigidx)
                ins = o.fn(engine)
                if o.is_dma:
                    ins.then_inc(dma_sems[o.ev_sem], 16)
                elif o.signal:
                    ins.then_inc(sems[o.eng], 1)
            for key, cnt in self.final_waits.get(engname, []):
                engine.wait_ge(dma_sems[key], cnt)

        block.tensor(lambda e: run("pe", e))
        block.scalar(lambda e: run("act", e))
        block.vector(lambda e: run("dve", e))
        block.gpsimd(lambda e: run("pool", e))
        block.sync(lambda e: run("sp", e))


class _Stop(Exception):
    pass


_LAST_DUMP_ORDER = []


def build_program(n_sb=NSB, n_layers=2, stop_after=None, dumps=None):
    DEBUG = dumps is not None
    del _LAST_DUMP_ORDER[:]
    nc = bass.Bass("TRN2", target_bir_lowering=False)
    dt_in = lambda name, shape: nc.dram_tensor(name, list(shape), F32, kind="ExternalInput").ap()
    xs = dt_in("xs", [NBLK * 128, D])
    w_in = dt_in("w_in", [2, D, 4608])
    w_bra = dt_in("w_br_att", [2, 512, D])
    w_brs = dt_in("w_br_sgu", [2, 512, D])
    w_out = dt_in("w_out", [2, D, D])
    w_fi = dt_in("w_ffn_in", [2, D, 2 * DFF])
    w_fo = dt_in("w_ffn_out", [2, DFF, D])
    gcols_d = dt_in("gcols", [128, 5, 8])
    lng_d = dt_in("lng", [128, 2, 512])
    lnb_d = dt_in("lnb", [128, 2, 512])
    bsb_d = dt_in("bsb", [128, 2, 512])
    wsT_d = dt_in("wsT", [128, 2, 4, 128])
    maskT_d = dt_in("maskT", [128, 128])
    biasg_d = dt_in("biasg", [2, 128, 8, 640])
    maskb_d = dt_in("maskb", [128, 640])
    bgate_d = dt_in("bgate", [128, 2, 2, 8])
    vones_d = dt_in("vones", [128, NBLK, 128])
    ident_d = dt_in("ident", [128, 128])
    y = nc.dram_tensor("y", [16 * 128, D], F32, kind="ExternalOutput").ap()

    dbg = nc.dram_tensor("dbg", [32, 128, D], F32, kind="ExternalOutput").ap() if DEBUG else None
    es = contextlib.ExitStack()
    with es:
        def sb(name, shape, dt):
            return es.enter_context(nc.sbuf_tensor("sb_" + name, list(shape), dt))

        h = sb("h", [128, 4, D], F32)
        Kr = [sb("Kr%d" % l, [128, 4, 1024], BF16) for l in range(2)]
        Vr = [sb("Vr%d" % l, [128, 8, 512], BF16) for l in range(2)]
        xnT = sb("xnT", [128, 8, 512], BF16)
        R = sb("R", [128, 16384], BF16)
        Wr = sb("Wr", [128, NS, 4096], BF16)
        biasb = sb("biasb", [128, 2, 8, 640], BF16)
        lng = sb("lng", [128, 2, 512], F32)
        lnb = sb("lnb", [128, 2, 512], F32)
        bsb = sb("bsb", [128, 2, 512], F32)
        wsTb = sb("wsTb", [128, 2, 4, 128], BF16)
        vonesb = sb("vonesb", [128, NBLK, 128], BF16)
        identb = sb("identb", [128, 128], BF16)
        gcols = sb("gcols", [128, 5, 8], F32)
        bgate = sb("bgate", [128, 2, 2, 8], F32)
        epsc = sb("epsc", [128, 1], F32)
        tinyc = sb("tinyc", [128, 1], F32)
        xn_tm = sb("xn_tm", [128, 2, D], BF16)
        sc = sb("sc", [128, 2, 640], F32)
        PT = sb("PT", [128, 3, 640], BF16)
        gv = sb("gv", [128, 4, 512], F32)
        gates = sb("gates", [128, 4, 512], F32)
        dtmp = sb("dtmp", [128, 512], F32)
        ss = sb("ss", [128, 4], F32)
        rstd = sb("rstd", [128, 4], F32)
        stats = sb("stats", [128, 4, 6], F32)
        mv = sb("mv", [128, 4, 2], F32)
        lrs = sb("lrs", [128, 4], F32)
        ps = es.enter_context(nc.psum_tensor("ps", [128, 8, 512], F32))
        psflat = ps[:].rearrange("p a b -> p (a b)")
        maskb = sc[:, 0, :]
        maskT = sc[:, 1, 0:128]
        junk = sc[:].rearrange("p a b -> p (a b)")[:, 0:1024]
        JUNK = [("sc", 0), ("sc", 1)]
        sg = gv[:, 0:2, :]
        gfin = gv[:, 2:4, :].rearrange("p a b -> p (a b)")

        Qz = R[:, 0:4096].rearrange("p (c t) -> p c t", c=8)
        uT = R[:, 4096:6144].rearrange("p (c t) -> p c t", c=4)
        vln = R[:, 6144:8192].rearrange("p (c t) -> p c t", c=4)
        attT = R[:, 8192:10240].rearrange("p (c t) -> p c t", c=4)
        sguT = R[:, 10240:12288].rearrange("p (c t) -> p c t", c=4)
        mrgT = R[:, 12288:16384].rearrange("p (c t) -> p c t", c=8)
        actT = R[:, 0:NFC * 512].rearrange("p (c t) -> p c t", c=NFC)
        RQ, RU, RV, RA, RS_, RM = "R_q", "R_u", "R_v", "R_a", "R_s", "R_m"
        R_ALL = [RQ, RU, RV, RA, RS_, RM]

        eng_sems = {e: es.enter_context(nc.semaphore("s_" + e)) for e in ENGS}
        dma_keys = ["c0", "c1", "c2"] + ["w%d" % i for i in range(NS)] + ["x%d" % i for i in range(4)] + \
                   ["o%d" % i for i in range(4)] + ["dbg"]
        dma_sems = {k: es.enter_context(nc.semaphore("d_" + k)) for k in dma_keys}
        block = es.enter_context(nc.Block())
        S = Sched()

        def pk(b):
            return [("ps", b)]

        S.op("sp", lambda e: e.dma_start(out=gcols[:], in_=gcols_d), writes=["gcols"], dma_sem="c0")
        S.op("sp", lambda e: e.dma_start(out=lng[:], in_=lng_d), writes=["lng"], dma_sem="c0")
        S.op("sp", lambda e: e.dma_start(out=lnb[:], in_=lnb_d), writes=["lnb"], dma_sem="c0")
        S.op("sp", lambda e: e.dma_start(out=bsb[:], in_=bsb_d), writes=["bsb"], dma_sem="c0")
        S.op("sp", lambda e: e.dma_start(out=bgate[:], in_=bgate_d), writes=["bgate"], dma_sem="c0")
        S.op("sp", lambda e: e.dma_start(out=maskT, in_=maskT_d), writes=[("sc", 1)], dma_sem="c0")
        S.op("sp", lambda e: e.dma_start(out=maskb, in_=maskb_d), writes=[("sc", 0)], dma_sem="c0")
        wsT_stage = gates[:].rearrange("p a b -> p (a b)")[:, 0:1024].rearrange("p (l g t) -> p l g t", l=2, g=4)
        S.op("sp", lambda e: e.dma_start(out=wsT_stage, in_=wsT_d), writes=[("gates", 0), ("gates", 1)], dma_sem="c0")
        S.dma_group_finalize("c0")
        S.op("pool", lambda e: e.dma_start(out=identb[:], in_=ident_d), writes=["identb"], dma_sem="c1")
        S.op("pool", lambda e: e.dma_start(out=vonesb[:], in_=vones_d), writes=["vonesb"], dma_sem="c1")
        S.dma_group_finalize("c1")
        S.op("dve", lambda e: e.memset(epsc[:], EPS), writes=["epsc"])
        S.op("dve", lambda e: e.memset(tinyc[:], 1e-20), writes=["tinyc"])
        S.op("dve", lambda e: e.tensor_tensor(out=wsTb[:].rearrange("p l g t -> p (l g) t"),
                                              in0=wsT_stage.rearrange("p l g t -> p (l g) t"),
                                              in1=maskT.unsqueeze(1).to_broadcast([128, 8, 128]), op=ALU.mult),
             reads=[("gates", 0), ("gates", 1), ("sc", 1)], writes=["wsTb"])
        Rf = R[:].bitcast(F32)
        bstage = Rf[:, 0:5120].rearrange("p (h c) -> p h c", h=8)
        for l in range(2):
            S.op("sp", (lambda l: lambda e: e.dma_start(out=bstage, in_=biasg_d[l]))(l), writes=R_ALL, dma_sem="c2")
            S.op("dve", (lambda l: lambda e: e.tensor_tensor(out=biasb[:, l, :, :], in0=bstage,
                                                             in1=maskb.unsqueeze(1).to_broadcast([128, 8, 640]),
                                                             op=ALU.add))(l),
                 reads=R_ALL + [("sc", 0)], writes=["biasb"])

        def units_for(l, kvonly):
            u = []

            def win(c0, n=512):
                return w_in[l][:, c0:c0 + n].rearrange("(c p) f -> p c f", p=128)

            def slot_view(kc, n):
                return (kc, n)

            if kvonly:
                u.append(("k", [((8, 512), win(512))]))
                u.append(("v", [((8, 512), win(1024))]))
                return u
            u.append(("vs", [((8, 512), win(2048))]))
            u.append(("u", [((8, 512), win(1536))]))
            u.append(("q", [((8, 512), win(0))]))
            u.append(("k", [((8, 512), win(512))]))
            u.append(("v", [((8, 512), win(1024))]))
            for hf in range(2):
                u.append(("br%d" % hf, [((4, 512, 0), w_bra[l][:, hf * 512:(hf + 1) * 512].rearrange("(c p) f -> p c f", p=128)),
                                        ((4, 512, 4), w_brs[l][:, hf * 512:(hf + 1) * 512].rearrange("(c p) f -> p c f", p=128))]))
                u.append(("g0%d" % hf, [((8, 512), win(2560 + hf * 512))]))
                u.append(("g1%d" % hf, [((8, 512), win(3584 + hf * 512))]))
            for hf in range(2):
                u.append(("wo%d" % hf, [((8, 512), w_out[l][:, hf * 512:(hf + 1) * 512].rearrange("(c p) f -> p c f", p=128))]))
            for i in range(6):
                n = 512 if i < 5 else 256
                u.append(("fg%d" % i, [((8, n), w_fi[l][:, i * 512:i * 512 + n].rearrange("(c p) f -> p c f", p=128))]))
                u.append(("fu%d" % i, [((8, n), w_fi[l][:, DFF + i * 512:DFF + i * 512 + n].rearrange("(c p) f -> p c f", p=128))]))
            for hf in range(2):
                for (c0, ncx) in ((0, 8), (8, 8), (16, 6)):
                    u.append(("fo%d_%d" % (hf, c0), [((ncx, 512), w_fo[l][c0 * 128:(c0 + ncx) * 128, hf * 512:(hf + 1) * 512]
                                                       .rearrange("(c p) f -> p c f", p=128))]))
            return u

        passes = []
        for s in range(n_sb):
            if s == 0:
                passes.append((0, s, True))
            else:
                passes.append((0, s, False))
                if n_layers > 1:
                    passes.append((1, s, s == 1))
        all_units = []
        for (l, s, kvonly) in passes:
            for (kind, dmas) in units_for(l, kvonly):
                all_units.append((l, s, kind, dmas))
        wstate = {"next_issue": 0, "next_use": 0}

        def issue_unit(n):
            l, s, kind, dmas = all_units[n]
            slot = n % NS
            for (spec, src) in dmas:
                if len(spec) == 2:
                    kc, ncol = spec
                    k0 = 0
                else:
                    kc, ncol, k0 = spec
                dst = Wr[:, slot, :].rearrange("p (c f) -> p c f", c=8)[:, k0:k0 + kc, 0:ncol] if ncol == 512 else \
                    Wr[:, slot, 0:8 * ncol].rearrange("p (c f) -> p c f", c=8)[:, k0:k0 + kc, :]
                S.op("pool", (lambda dst, src: lambda e: e.dma_start(out=dst, in_=src))(dst, src),
                     writes=[("w", slot, 0), ("w", slot, 1)], dma_sem="w%d" % slot)
            S.dma_group_finalize("w%d" % slot)

        def w_prefetch():
            while wstate["next_issue"] < len(all_units) and wstate["next_issue"] < wstate["next_use"] + NS:
                issue_unit(wstate["next_issue"])
                wstate["next_issue"] += 1

        def w_acquire(l, s, kind):
            n = wstate["next_use"]
            ul, us, ukind, dmas = all_units[n]
            assert (ul, us, ukind) == (l, s, kind), ((ul, us, ukind), (l, s, kind))
            assert wstate["next_issue"] > n
            wstate["next_use"] += 1
            slot = n % NS
            ncol = dmas[0][0][1]
            view = Wr[:, slot, 0:8 * ncol].rearrange("p (c f) -> p c f", c=8)
            return view, [("w", slot, 0), ("w", slot, 1)]

        def w_release():
            w_prefetch()

        w_prefetch()

        def blk_slot(b):
            return b % 8

        def load_x(s):
            for j in range(4):
                b = 4 * s + j
                S.op("sp", (lambda b, j: lambda e: e.dma_start(out=h[:, j, :], in_=xs[b * 128:(b + 1) * 128, :]))(b, j),
                     writes=[("h", j)], dma_sem="x%d" % j)

        tp_bank = [6]

        def norm_stats(j):
            S.op("act", lambda e: e.activation(out=junk, in_=h[:, j, :], func=AF.Square, accum_out=ss[:, j:j + 1]),
                 reads=[("h", j)], writes=JUNK + [("ss", j)])
            S.op("act", lambda e: e.activation(out=rstd[:, j:j + 1], in_=ss[:, j:j + 1], func=AF.Sqrt, scale=1.0 / D, bias=epsc[:]),
                 reads=[("ss", j), "epsc"], writes=[("rstd", j)])
            S.op("dve", lambda e: e.reciprocal(out=rstd[:, j:j + 1], in_=rstd[:, j:j + 1]), reads=[("rstd", j)], writes=[("rstd", j)])

        def stage_norm(gi):
            for j in range(4):
                norm_stats(j)
            banks = []

            def scale_and_transpose(j):
                xb = j % 2
                if j % 2 == 0:
                    S.op("dve", lambda e: e.tensor_scalar(out=xn_tm[:, xb, :], in0=h[:, j, :], scalar1=rstd[:, j:j + 1],
                                                          scalar2=None, op0=ALU.mult),
                         reads=[("h", j), ("rstd", j)], writes=[("xn_tm", xb)])
                else:
                    S.op("act", lambda e: e.activation(out=xn_tm[:, xb, :], in_=h[:, j, :], func=AF.Copy, scale=rstd[:, j:j + 1]),
                         reads=[("h", j), ("rstd", j)], writes=[("xn_tm", xb)])
                bank = tp_bank[0]
                tp_bank[0] = 6 if bank == 7 else 7
                pT = ps[:, bank, :].bitcast(BF16).rearrange("p (c t) -> p c t", c=8)
                for c in range(8):
                    S.op("pe", (lambda c, pT: lambda e: e.transpose(out=pT[:, c, :], in_=xn_tm[:, xb, c * 128:(c + 1) * 128],
                                                                     identity=identb[:]))(c, pT),
                         reads=[("xn_tm", xb), "identb"], writes=pk(bank))
                banks.append((bank, pT))

            def evac(j):
                bank, pT = banks[j]
                S.op("dve", lambda e: e.tensor_tensor(
                    out=xnT[:, :, j * 128:(j + 1) * 128], in0=pT,
                    in1=gcols[:, gi, :].unsqueeze(2).to_broadcast([128, 8, 128]), op=ALU.mult),
                    reads=pk(bank) + ["gcols"], writes=[("xnT", j)])

            scale_and_transpose(0)
            for j in range(1, 4):
                scale_and_transpose(j)
                evac(j - 1)
            evac(3)

        XNT_ALL = [("xnT", j) for j in range(4)]
        acc_bank = [0]

        def next_bank(lo=0, n=4):
            b = lo + acc_bank[0] % n
            acc_bank[0] += 1
            return b

        def proj_fm(l, s, kind, evac):
            wv, wkey = w_acquire(l, s, kind)
            for fc in range(4):
                bank = next_bank()
                for k in range(8):
                    S.op("pe", (lambda fc, k, bank: lambda e: e.matmul(ps[:, bank, :], lhsT=wv[:, k, fc * 128:(fc + 1) * 128],
                                                                       rhs=xnT[:, k, :], start=(k == 0), stop=(k == 7)))(fc, k, bank),
                         reads=wkey + XNT_ALL, writes=pk(bank))
                evac(fc, bank)
            w_release()

        def proj_tm(l, s, kind, evac):
            wv, wkey = w_acquire(l, s, kind)
            for j in range(4):
                bank = next_bank()
                for k in range(8):
                    S.op("pe", (lambda j, k, bank: lambda e: e.matmul(ps[:, bank, :], lhsT=xnT[:, k, j * 128:(j + 1) * 128],
                                                                      rhs=wv[:, k, :], start=(k == 0), stop=(k == 7)))(j, k, bank),
                         reads=wkey + [("xnT", j)], writes=pk(bank))
                evac(j, bank)
            w_release()

        def stage_proj(l, s, kvonly):
            koff = ((4 * s) % 8) * 128

            def ev_q(fc, bank):
                if fc == 0:
                    S.op("pool", lambda e: e.memset(Qz[64:128, 0:8:2, :], 0.0), writes=[RQ])
                    S.op("pool", lambda e: e.memset(Qz[0:64, 1:8:2, :], 0.0), writes=[RQ])
                S.op("act", lambda e: e.activation(out=Qz[0:64, 2 * fc, :], in_=ps[0:64, bank, :], func=AF.Copy, scale=0.125),
                     reads=pk(bank), writes=[RQ, ("Qh", 2 * fc)])
                S.op("act", lambda e: e.activation(out=Qz[64:128, 2 * fc + 1, :], in_=ps[64:128, bank, :], func=AF.Copy, scale=0.125),
                     reads=pk(bank), writes=[RQ, ("Qh", 2 * fc + 1)])

            def ev_k(fc, bank):
                S.op("dve", lambda e: e.tensor_copy(out=Kr[l][:, fc, koff:koff + 512], in_=ps[:, bank, :]),
                     reads=pk(bank), writes=[("Kr", l, (4 * s) % 8 // 4)])

            def ev_v(j, bank):
                slot = blk_slot(4 * s + j)
                S.op("dve", lambda e: e.tensor_copy(out=Vr[l][:, slot, :], in_=ps[:, bank, :]),
                     reads=pk(bank), writes=[("Vr", l, slot)])

            def ev_u(fc, bank):
                S.op("act", lambda e: e.activation(out=uT[:, fc, :], in_=ps[:, bank, :], func=AF.Gelu_apprx_tanh),
                     reads=pk(bank), writes=[RU])

            def ev_vs(j, bank):
                S.op("act", lambda e: e.activation(out=gv[:, j, :], in_=ps[:, bank, :], func=AF.Gelu_apprx_tanh),
                     reads=pk(bank), writes=[("gv", j)])
                S.op("dve", lambda e: e.bn_stats(out=stats[:, j, :], in_=gv[:, j, :]), reads=[("gv", j)], writes=[("stats", j)])
                S.op("dve", lambda e: e.bn_aggr(out=mv[:, j, :], in_=stats[:, j, :]), reads=[("stats", j)], writes=[("mv", j)])

            if kvonly:
                proj_fm(l, s, "k", ev_k)
                proj_tm(l, s, "v", ev_v)
                return
            proj_tm(l, s, "vs", ev_vs)
            proj_fm(l, s, "u", ev_u)
            S.op("act", lambda e: e.activation(out=lrs[:], in_=mv[:, :, 1], func=AF.Sqrt, scale=1.0, bias=epsc[:]),
                 reads=[("mv", j) for j in range(4)] + ["epsc"], writes=["lrs"])
            S.op("dve", lambda e: e.reciprocal(out=lrs[:], in_=lrs[:]), reads=["lrs"], writes=["lrs"])
            for j in range(4):
                S.op("dve", (lambda j: lambda e: e.tensor_scalar(out=gv[:, j, :], in0=gv[:, j, :], scalar1=mv[:, j, 0:1],
                                                                  scalar2=lrs[:, j:j + 1], op0=ALU.subtract, op1=ALU.mult))(j),
                     reads=[("gv", j), ("mv", j), "lrs"], writes=[("gv", j)])
                S.op("dve", (lambda j: lambda e: e.tensor_tensor(out=gv[:, j, :], in0=gv[:, j, :], in1=lng[:, l, :], op=ALU.mult))(j),
                     reads=[("gv", j), "lng"], writes=[("gv", j)])
                S.op("dve", (lambda j: lambda e: e.tensor_tensor(out=vln[:, j, :], in0=gv[:, j, :], in1=lnb[:, l, :], op=ALU.add))(j),
                     reads=[("gv", j), "lnb"], writes=[RV])
            proj_fm(l, s, "q", ev_q)
            proj_fm(l, s, "k", ev_k)
            proj_tm(l, s, "v", ev_v)

        sbuf_ctr = {"sA": 0, "sc": 0, "pt": 0, "dt": 0}

        def stage_attn(l, s):
            units = [(j, hh) for j in range(4) for hh in range(8)]
            st = {}

            def emit_qk(n):
                j, hh = units[n]
                p = hh // 2
                b = 4 * s + j
                kslots = [blk_slot(b - 4 + i) for i in range(5)]
                kreads = [("Kr", l, ks // 4) for ks in set(kslots)]
                sbi = sbuf_ctr["sA"] % 3
                sbuf_ctr["sA"] += 1
                base = sbi * 1024
                banks = [2 * sbi, 2 * sbi + 1]
                for i in range(5):
                    ks = kslots[i]
                    col = base + i * 128
                    S.op("pe", (lambda col, ks, p, hh, j: lambda e: e.matmul(
                        psflat[:, col:col + 128], lhsT=Kr[l][:, p, ks * 128:(ks + 1) * 128],
                        rhs=Qz[:, hh, j * 128:(j + 1) * 128], start=True, stop=True))(col, ks, p, hh, j),
                        reads=kreads + [RQ, ("Qh", hh)], writes=pk(col // 512))
                sci = sbuf_ctr["sc"] % 2
                sbuf_ctr["sc"] += 1
                S.op("dve", (lambda base, hh, sci: lambda e: e.tensor_tensor(
                    out=sc[:, sci, :], in0=psflat[:, base:base + 640], in1=biasb[:, l, hh, :], op=ALU.add))(base, hh, sci),
                    reads=pk(banks[0]) + pk(banks[1]) + ["biasb"], writes=[("sc", sci)])
                pi = sbuf_ctr["pt"] % 3
                sbuf_ctr["pt"] += 1
                S.op("act", (lambda sci, pi: lambda e: e.activation(out=PT[:, pi, :], in_=sc[:, sci, :], func=AF.Exp))(sci, pi),
                     reads=[("sc", sci)], writes=[("PT", pi)])
                st[n] = (kslots, pi)

            def emit_pv(n):
                j, hh = units[n]
                p, par = hh // 2, hh % 2
                b = 4 * s + j
                kslots, pi = st.pop(n)
                bank = 6 + p % 2
                for kind in ("num", "den"):
                    cc = par * 128 + (0 if kind == "num" else 256)
                    for i in range(5):
                        ks = kslots[i]
                        if kind == "num":
                            lhs = Vr[l][:, ks, p * 128:(p + 1) * 128]
                            rd = [("Vr", l, ks)]
                        else:
                            lhs = vonesb[:, b - 4 + i, :]
                            rd = ["vonesb"]
                        S.op("pe", (lambda lhs, pi, i, bank, cc: lambda e: e.matmul(
                            ps[:, bank, cc:cc + 128], lhsT=lhs,
                            rhs=PT[:, pi, i * 128:(i + 1) * 128], start=(i == 0), stop=(i == 4)))(lhs, pi, i, bank, cc),
                            reads=rd + [("PT", pi)], writes=pk(bank))
                if par != 1:
                    return None

                def norm():
                    di = sbuf_ctr["dt"] % 2
                    sbuf_ctr["dt"] += 1
                    dsl = dtmp[:, di * 256:(di + 1) * 256]
                    S.op("act", (lambda bank, dsl: lambda e: e.activation(out=dsl, in_=ps[:, bank, 256:512], func=AF.Ln,
                                                                          bias=tinyc[:]))(bank, dsl),
                         reads=pk(bank) + ["tinyc"], writes=[("dtmp", di)])
                    S.op("act", (lambda dsl: lambda e: e.activation(out=dsl, in_=dsl, func=AF.Exp, scale=-1.0))(dsl),
                         reads=[("dtmp", di)], writes=[("dtmp", di)])
                    for hf2 in range(2):
                        rs = slice(hf2 * 64, (hf2 + 1) * 64)
                        S.op("dve", (lambda j, p, bank, dsl, rs, hf2: lambda e: e.tensor_tensor(
                            out=attT[rs, p, j * 128:(j + 1) * 128], in0=ps[rs, bank, hf2 * 128:(hf2 + 1) * 128],
                            in1=dsl[rs, hf2 * 128:(hf2 + 1) * 128], op=ALU.mult))(j, p, bank, dsl, rs, hf2),
                            reads=pk(bank) + [("dtmp", di)], writes=[RA])
                return norm

            nu = len(units)
            pending = []
            for n in range(min(ATT_SKEW, nu)):
                emit_qk(n)
            for n in range(nu):
                if n + ATT_SKEW < nu:
                    emit_qk(n + ATT_SKEW)
                f = emit_pv(n)
                if f is not None:
                    pending.append(f)
                if len(pending) > 1:
                    pending.pop(0)()
            while pending:
                pending.pop(0)()

        def stage_sgu(l, s):
            for j in range(4):
                bank = next_bank()
                for g in range(4):
                    S.op("pe", (lambda j, g, bank: lambda e: e.matmul(ps[:, bank, g * 128:(g + 1) * 128],
                                                                      lhsT=vln[:, j, g * 128:(g + 1) * 128],
                                                                      rhs=wsTb[:, l, g, :], start=True, stop=True))(j, g, bank),
                         reads=[RV, "wsTb"], writes=pk(bank))
                gb = gates[:, j % 2, :]
                S.op("dve", (lambda bank, gb: lambda e: e.tensor_tensor(out=gb, in0=ps[:, bank, :], in1=bsb[:, l, :], op=ALU.add))(bank, gb),
                     reads=pk(bank) + ["bsb"], writes=[("gates", j % 2)])
                S.op("dve", (lambda j, gb: lambda e: e.tensor_tensor(
                    out=sguT[:, :, j * 128:(j + 1) * 128], in0=gb.rearrange("p (c t) -> p c t", c=4),
                    in1=uT[:, :, j * 128:(j + 1) * 128], op=ALU.mult))(j, gb),
                    reads=[("gates", j % 2), RU], writes=[RS_])

        def stage_merge(l, s):
            for hf in range(2):
                wbr, kbr = w_acquire(l, s, "br%d" % hf)
                wg0, kg0 = w_acquire(l, s, "g0%d" % hf)
                wg1, kg1 = w_acquire(l, s, "g1%d" % hf)
                for f4 in range(4):
                    fc = hf * 4 + f4
                    bset = (fc % 2) * 4
                    bA, bB, bG0, bG1 = bset, bset + 1, bset + 2, bset + 3
                    cs = slice(f4 * 128, (f4 + 1) * 128)
                    for k in range(4):
                        S.op("pe", (lambda k, bA, cs, wbr: lambda e: e.matmul(ps[:, bA, :], lhsT=wbr[:, k, cs], rhs=attT[:, k, :],
                                                                              start=(k == 0), stop=(k == 3)))(k, bA, cs, wbr),
                             reads=kbr + [RA], writes=pk(bA))
                    for k in range(4):
                        S.op("pe", (lambda k, bB, cs, wbr: lambda e: e.matmul(ps[:, bB, :], lhsT=wbr[:, 4 + k, cs], rhs=sguT[:, k, :],
                                                                              start=(k == 0), stop=(k == 3)))(k, bB, cs, wbr),
                             reads=kbr + [RS_], writes=pk(bB))
                    for (wg, kg, bG, gi) in ((wg0, kg0, bG0, 0), (wg1, kg1, bG1, 1)):
                        for k in range(8):
                            S.op("pe", (lambda k, bG, cs, wg: lambda e: e.matmul(ps[:, bG, :], lhsT=wg[:, k, cs], rhs=xnT[:, k, :],
                                                                                 start=(k == 0), stop=(k == 7)))(k, bG, cs, wg),
                                 reads=kg + XNT_ALL, writes=pk(bG))
                        gidx = (fc % 2) * 2 + gi
                        S.op("act", (lambda bG, gidx, gi, fc: lambda e: e.activation(
                            out=gates[:, gidx, :], in_=ps[:, bG, :], func=AF.Sigmoid, bias=bgate[:, l, gi, fc:fc + 1]))(bG, gidx, gi, fc),
                            reads=pk(bG) + ["bgate"], writes=[("gates", gidx)])
                    g0i, g1i = (fc % 2) * 2, (fc % 2) * 2 + 1
                    S.op("dve", (lambda bA, g0i: lambda e: e.tensor_tensor(out=gates[:, g0i, :], in0=ps[:, bA, :], in1=gates[:, g0i, :],
                                                                           op=ALU.mult))(bA, g0i),
                         reads=pk(bA) + [("gates", g0i)], writes=[("gates", g0i)])
                    S.op("dve", (lambda bB, g1i: lambda e: e.tensor_tensor(out=gates[:, g1i, :], in0=ps[:, bB, :], in1=gates[:, g1i, :],
                                                                           op=ALU.mult))(bB, g1i),
                         reads=pk(bB) + [("gates", g1i)], writes=[("gates", g1i)])
                    S.op("dve", (lambda fc, g0i, g1i: lambda e: e.tensor_tensor(out=mrgT[:, fc, :], in0=gates[:, g0i, :],
                                                                                in1=gates[:, g1i, :], op=ALU.add))(fc, g0i, g1i),
                         reads=[("gates", g0i), ("gates", g1i)], writes=[RM])
                w_release()
                w_release()
                w_release()

        def stage_wout(l, s):
            for hf in range(2):
                wv, wkey = w_acquire(l, s, "wo%d" % hf)
                for j in range(4):
                    bank = next_bank()
                    for k in range(8):
                        S.op("pe", (lambda j, k, bank, wv: lambda e: e.matmul(ps[:, bank, :], lhsT=mrgT[:, k, j * 128:(j + 1) * 128],
                                                                              rhs=wv[:, k, :], start=(k == 0), stop=(k == 7)))(j, k, bank, wv),
                             reads=wkey + [RM], writes=pk(bank))
                    S.op("dve", (lambda j, bank, hf: lambda e: e.tensor_tensor(out=h[:, j, hf * 512:(hf + 1) * 512], in0=ps[:, bank, :],
                                                                               in1=h[:, j, hf * 512:(hf + 1) * 512], op=ALU.add))(j, bank, hf),
                         reads=pk(bank) + [("h", j)], writes=[("h", j)])
                w_release()

        def stage_ffn_in(l, s):
            for i in range(6):
                wg, kg = w_acquire(l, s, "fg%d" % i)
                wu, ku = w_acquire(l, s, "fu%d" % i)
                nch = 4 if i < 5 else 2
                for c4 in range(nch):
                    c = i * 4 + c4
                    bset = (c % 4) * 2
                    bG, bU = bset, bset + 1
                    cs = slice(c4 * 128, (c4 + 1) * 128)
                    for (wv, wk, bank) in ((wg, kg, bG), (wu, ku, bU)):
                        for k in range(8):
                            S.op("pe", (lambda k, bank, cs, wv: lambda e: e.matmul(ps[:, bank, :], lhsT=wv[:, k, cs], rhs=xnT[:, k, :],
                                                                                   start=(k == 0), stop=(k == 7)))(k, bank, cs, wv),
                                 reads=wk + XNT_ALL, writes=pk(bank))
                    si = c % 2
                    S.op("act", (lambda bG, si: lambda e: e.activation(out=sg[:, si, :], in_=ps[:, bG, :], func=AF.Silu))(bG, si),
                         reads=pk(bG), writes=[("gv", si)])
                    S.op("dve", (lambda bU, si, c: lambda e: e.tensor_tensor(out=actT[:, c, :], in0=ps[:, bU, :], in1=sg[:, si, :],
                                                                             op=ALU.mult))(bU, si, c),
                         reads=pk(bU) + [("gv", si)], writes=R_ALL)
                w_release()
                w_release()

        def stage_ffn_out(l, s):
            for hf in range(2):
                ws = []
                for (c0, ncx) in ((0, 8), (8, 8), (16, 6)):
                    wv, wkey = w_acquire(l, s, "fo%d_%d" % (hf, c0))
                    ws.append((c0, ncx, wv, wkey))
                for j in range(4):
                    bank = next_bank()
                    for (c0, ncx, wv, wkey) in ws:
                        for cc in range(ncx):
                            c = c0 + cc
                            S.op("pe", (lambda j, c, cc, bank, wv: lambda e: e.matmul(
                                ps[:, bank, :], lhsT=actT[:, c, j * 128:(j + 1) * 128], rhs=wv[:, cc, :],
                                start=(c == 0), stop=(c == NFC - 1)))(j, c, cc, bank, wv),
                                reads=wkey + R_ALL, writes=pk(bank))
                    S.op("dve", (lambda j, bank, hf: lambda e: e.tensor_tensor(out=h[:, j, hf * 512:(hf + 1) * 512], in0=ps[:, bank, :],
                                                                               in1=h[:, j, hf * 512:(hf + 1) * 512], op=ALU.add))(j, bank, hf),
                         reads=pk(bank) + [("h", j)], writes=[("h", j)])
                w_release()
                w_release()
                w_release()

        def stage_final(s):
            S.op("sp", lambda e: e.dma_start(out=gfin, in_=gfin_d), writes=[("gv", 2), ("gv", 3)], dma_sem="c0")
            S.dma_group_finalize("c0")
            yst = R[:].bitcast(F32)[:, 0:4 * D].rearrange("p (j f) -> p j f", j=4)
            for j in range(4):
                norm_stats(j)
                S.op("dve", (lambda j: lambda e: e.scalar_tensor_tensor(
                    out=yst[:, j, :], in0=h[:, j, :], scalar=rstd[:, j:j + 1], in1=gfin, op0=ALU.mult, op1=ALU.mult))(j),
                    reads=[("h", j), ("rstd", j), ("gv", 2), ("gv", 3)], writes=(R_ALL if j == 0 else []) + [("yst", j)])
                ob = (4 * s + j) - 8
                S.op("sp", (lambda j, ob: lambda e: e.dma_start(out=y[ob * 128:(ob + 1) * 128, :], in_=yst[:, j, :]))(j, ob),
                     reads=R_ALL + [("yst", j)], dma_sem="o%d" % j)

        gfin_d = dt_in("gfin", [128, D])

        dstage = sb("dstage", [128, D], F32) if DEBUG else None
        dctr = [0]

        def dump(name, ap, reads, n=D):
            if not DEBUG or name not in dumps:
                return
            idx = dctr[0]
            dctr[0] += 1
            _LAST_DUMP_ORDER.append(name)
            S.op("dve", lambda e: e.tensor_copy(out=dstage[:, 0:n], in_=ap), reads=reads, writes=["dstage"])
            S.op("sp", lambda e: e.dma_start(out=dbg[idx][:, 0:n], in_=dstage[:, 0:n]), reads=["dstage"], dma_sem="dbg")

        def chk(tag):
            if stop_after == tag:
                raise _Stop()

        def run_layer(l, s, kvonly):
            T = "L%d_S%d_" % (l, s)
            stage_norm(2 * l)
            dump(T + "xnT0", xnT[:, 0, :], XNT_ALL, 512)
            dump(T + "xnT7", xnT[:, 7, :], XNT_ALL, 512)
            chk(T + "norm")
            stage_proj(l, s, kvonly)
            dump(T + "K0", Kr[l][:, 0, :], [("Kr", l, 0), ("Kr", l, 1)], 1024)
            dump(T + "V0", Vr[l][:, blk_slot(4 * s), :], [("Vr", l, blk_slot(4 * s))], 512)
            if kvonly:
                chk(T + "proj")
                return
            dump(T + "q0", Qz[:, 0, :], [RQ, ("Qh", 0)], 512)
            dump(T + "u0", uT[:, 0, :], [RU], 512)
            dump(T + "vln0", vln[:, 0, :], [RV], 512)
            chk(T + "proj")
            stage_attn(l, s)
            dump(T + "att0", attT[:, 0, :], [RA], 512)
            dump(T + "att3", attT[:, 3, :], [RA], 512)
            chk(T + "attn")
            stage_sgu(l, s)
            dump(T + "sgu0", sguT[:, 0, :], [RS_], 512)
            chk(T + "sgu")
            stage_merge(l, s)
            for fcx in range(8):
                dump(T + "mrg%d" % fcx, mrgT[:, fcx, :], [RM], 512)
            chk(T + "merge")
            stage_wout(l, s)
            dump(T + "h0mid", h[:, 0, :], [("h", 0)])
            chk(T + "wout")
            stage_norm(2 * l + 1)
            stage_ffn_in(l, s)
            dump(T + "act0", actT[:, 0, :], R_ALL, 512)
            dump(T + "act21", actT[:, 21, :], R_ALL, 512)
            chk(T + "ffn_in")
            stage_ffn_out(l, s)
            dump(T + "h0", h[:, 0, :], [("h", 0)])
            dump(T + "h3", h[:, 3, :], [("h", 3)])
            chk(T + "ffn_out")

        try:
            for s in range(n_sb):
                load_x(s)
                run_layer(0, s, s == 0)
                if s >= 1 and n_layers > 1:
                    run_layer(1, s, s == 1)
                if s >= 2:
                    stage_final(s)
            assert wstate["next_use"] == len(all_units), (wstate, len(all_units))
        except _Stop:
            pass

        S.final_waits["sp"] = [("o%d" % j, S.dma_counts.get("o%d" % j, 0)) for j in range(4)]
        S.final_waits["pool"] = [(k, v) for k, v in S.dma_counts.items() if k.startswith("w") or k == "c1"]
        S.final_waits["sp"] += [(k, v) for k, v in S.dma_counts.items() if k.startswith("x") or k in ("c0", "c2")]
        if DEBUG:
            S.final_waits["sp"].append(("dbg", S.dma_counts.get("dbg", 0)))
        S.emit(block, eng_sems, dma_sems)
    return nc


def _host_consts(att_rel_bias, sgu_norm_gain, sgu_norm_bias, sgu_w, sgu_b, b_gate, norm_mix, norm_ffn, norm_final):
    f32 = np.float32
    gl = [norm_mix[0], norm_ffn[0], norm_mix[1], norm_ffn[1], norm_final]
    gcols = np.stack([np.asarray(g, f32).reshape(8, 128).T for g in gl], axis=1)
    lng = np.broadcast_to(np.asarray(sgu_norm_gain, f32)[None], (128, 2, 512)).copy()
    lnb = np.broadcast_to(np.asarray(sgu_norm_bias, f32)[None], (128, 2, 512)).copy()
    bsb = np.broadcast_to(np.asarray(sgu_b, f32).reshape(2, 512)[None], (128, 2, 512)).copy()
    wsT = np.ascontiguousarray(np.asarray(sgu_w, f32).transpose(3, 0, 1, 2))
    maskT = (np.arange(128)[:, None] <= np.arange(128)[None, :]).astype(f32)
    ki = np.arange(128)[:, None, None]
    i5 = np.arange(5)[None, :, None]
    qi = np.arange(128)[None, None, :]
    rel = np.clip((4 - i5) * 128 + qi - ki, -128, 128) + 128
    biasg = np.asarray(att_rel_bias, f32)[:, :, rel]
    biasg = np.ascontiguousarray(biasg.transpose(0, 2, 1, 3, 4)).reshape(2, 128, 8, 640)
    dchunk = (8 - 2 * i5) + (qi // 64) - (ki // 64)
    valid = (dchunk >= 0) & (dchunk <= 8)
    maskb = np.where(valid, 0.0, NEG).astype(f32).reshape(128, 640)
    bgate = np.ascontiguousarray(np.asarray(b_gate, f32).reshape(2, 2, 8, 128).transpose(3, 0, 1, 2))
    gfin = np.broadcast_to(np.asarray(norm_final, f32)[None], (128, D)).copy()
    ident = np.eye(128, dtype=f32)
    return dict(gcols=gcols, lng=lng, lnb=lnb, bsb=bsb, wsT=wsT, maskT=maskT, biasg=biasg, maskb=maskb,
                bgate=bgate, gfin=gfin, ident=ident)


def _make_in_maps(x, shared):
    f32 = np.float32
    in_maps = []
    for c in range(8):
        b, half = c // 2, c % 2
        xs = np.zeros((NBLK * 128, D), f32)
        vones = np.ones((128, NBLK, 128), f32)
        if half == 0:
            xs[1024:] = x[b, 0:2048]
            vones[:, 0:8, :] = 0.0
        else:
            xs[:] = x[b, 1024:4096]
        m = dict(shared)
        m["xs"] = xs
        m["vones"] = vones
        in_maps.append(m)
    return in_maps


_PROGRAM = {}


def kernel(x, norm_mix, w_in, att_rel_bias, sgu_norm_gain, sgu_norm_bias, sgu_w, sgu_b,
           w_br_att, w_br_sgu, b_gate, w_out, norm_ffn, w_ffn_in, w_ffn_out, norm_final):
    f32 = np.float32
    x = np.asarray(x, f32)
    consts = _host_consts(att_rel_bias, sgu_norm_gain, sgu_norm_bias, sgu_w, sgu_b, b_gate, norm_mix, norm_ffn, norm_final)
    shared = dict(w_in=np.ascontiguousarray(w_in, f32), w_br_att=np.ascontiguousarray(w_br_att, f32),
                  w_br_sgu=np.ascontiguousarray(w_br_sgu, f32), w_out=np.ascontiguousarray(w_out, f32),
                  w_ffn_in=np.ascontiguousarray(w_ffn_in, f32), w_ffn_out=np.ascontiguousarray(w_ffn_out, f32))
    shared.update(consts)
    in_maps = _make_in_maps(x, shared)
    if "nc" not in _PROGRAM:
        _PROGRAM["nc"] = build_program()
    res = run_bass_kernel_spmd(_PROGRAM["nc"], in_maps, core_ids=list(range(8)))
    out = np.empty((4, 4096, D), f32)
    for c in range(8):
        b, half = c // 2, c % 2
        out[b, half * 2048:(half + 1) * 2048] = res.results[c]["y"]
    return out
```

```python
import contextlib
import numpy as np
import concourse.bass as bass
import concourse.mybir as mybir
from concourse.bass_utils import run_bass_kernel_spmd

F32 = mybir.dt.float32
BF16 = mybir.dt.bfloat16
AF = mybir.ActivationFunctionType
ALU = mybir.AluOpType

D = 1024
DFF = 2816
NFC = 22
NBLK = 24
NSB = 6
NS = 6
EPS = 1e-6
NEG = -30000.0
ATT_SKEW = 2

ENGS = ("pe", "act", "dve", "pool", "sp")


class Op:
    __slots__ = ("eng", "idx", "fn", "waits", "signal", "ev_sem", "ev_val", "is_dma", "sigidx")

    def __init__(self, eng, idx, fn, is_dma):
        self.eng = eng
        self.idx = idx
        self.fn = fn
        self.waits = []
        self.signal = False
        self.is_dma = is_dma
        self.ev_sem = None
        self.ev_val = None
        self.sigidx = None


class Sched:
    def __init__(self):
        self.ops = {e: [] for e in ENGS}
        self.last_w = {}
        self.readers = {}
        self.dma_counts = {}
        self.waited = {e: {} for e in ENGS}
        self.final_waits = {}
        self.group_ops = {}

    def _add_dep(self, op, dep):
        if dep is None or dep is op:
            return
        if dep.is_dma:
            key = ("dma", dep.ev_sem)
            prev = self.waited[op.eng].get(key, 0)
            if dep.ev_val <= prev:
                return
            self.waited[op.eng][key] = dep.ev_val
            op.waits.append((dep.ev_sem, dep.ev_val))
            return
        if dep.eng == op.eng:
            return
        prev = self.waited[op.eng].get(dep.eng, -1)
        if dep.idx <= prev:
            return
        self.waited[op.eng][dep.eng] = dep.idx
        dep.signal = True
        op.waits.append(dep)

    def op(self, eng, fn, reads=(), writes=(), dma_sem=None):
        is_dma = dma_sem is not None
        excl = [b for b in reads if isinstance(b, tuple) and b[0] == "ps"]
        if excl:
            reads = [b for b in reads if not (isinstance(b, tuple) and b[0] == "ps")]
            writes = list(writes) + excl
        o = Op(eng, len(self.ops[eng]), fn, is_dma)
        if is_dma:
            c = self.dma_counts.get(dma_sem, 0) + 16
            self.dma_counts[dma_sem] = c
            o.ev_sem = dma_sem
            o.ev_val = c
            self.group_ops.setdefault(dma_sem, []).append(o)
        for b in reads:
            w = self.last_w.get(b)
            if w is not None:
                if (not w.is_dma) and w.eng == eng and not is_dma:
                    if eng != "pe" and o.idx - w.idx <= 2:
                        w.signal = True
                        if w not in o.waits:
                            o.waits.append(w)
                else:
                    self._add_dep(o, w)
        for b in writes:
            w = self.last_w.get(b)
            if w is not None:
                self._add_dep(o, w)
            for r in self.readers.get(b, {}).values():
                self._add_dep(o, r)
        for b in reads:
            self.readers.setdefault(b, {})[(eng, o.ev_sem) if is_dma else eng] = o
        for b in writes:
            self.last_w[b] = o
            self.readers[b] = {}
        self.ops[eng].append(o)
        return o

    def dma_group_finalize(self, key):
        tot = self.dma_counts.get(key, 0)
        for o in self.group_ops.get(key, []):
            o.ev_val = tot
        self.group_ops[key] = []

    def emit(self, block, sems, dma_sems):
        for e in ENGS:
            n = 0
            for o in self.ops[e]:
                if (not o.is_dma) and o.signal:
                    n += 1
                    o.sigidx = n

        def run(engname, engine):
            for o in self.ops[engname]:
                for d in o.waits:
                    if isinstance(d, tuple):
                        engine.wait_ge(dma_sems[d[0]], d[1])
                    else:
                        engine.wait_ge(sems[d.eng], d.sigidx)
                ins = o.fn(engine)
                if o.is_dma:
                    ins.then_inc(dma_sems[o.ev_sem], 16)
                elif o.signal:
                    ins.then_inc(sems[o.eng], 1)
            for key, cnt in self.final_waits.get(engname, []):
                engine.wait_ge(dma_sems[key], cnt)

        block.tensor(lambda e: run("pe", e))
        block.scalar(lambda e: run("act", e))
        block.vector(lambda e: run("dve", e))
        block.gpsimd(lambda e: run("pool", e))
        block.sync(lambda e: run("sp", e))


class _Stop(Exception):
    pass


_LAST_DUMP_ORDER = []


def build_program(n_sb=NSB, n_layers=2, stop_after=None, dumps=None):
    DEBUG = dumps is not None
    del _LAST_DUMP_ORDER[:]
    nc = bass.Bass("TRN2", target_bir_lowering=False)
    dt_in = lambda name, shape: nc.dram_tensor(name, list(shape), F32, kind="ExternalInput").ap()
    xs = dt_in("xs", [NBLK * 128, D])
    w_in = dt_in("w_in", [2, D, 4608])
    w_bra = dt_in("w_br_att", [2, 512, D])
    w_brs = dt_in("w_br_sgu", [2, 512, D])
    w_out = dt_in("w_out", [2, D, D])
    w_fi = dt_in("w_ffn_in", [2, D, 2 * DFF])
    w_fo = dt_in("w_ffn_out", [2, DFF, D])
    gcols_d = dt_in("gcols", [128, 5, 8])
    lng_d = dt_in("lng", [128, 2, 512])
    lnb_d = dt_in("lnb", [128, 2, 512])
    bsb_d = dt_in("bsb", [128, 2, 512])
    wsT_d = dt_in("wsT", [128, 2, 4, 128])
    maskT_d = dt_in("maskT", [128, 128])
    biasg_d = dt_in("biasg", [2, 128, 8, 640])
    maskb_d = dt_in("maskb", [128, 640])
    bgate_d = dt_in("bgate", [128, 2, 2, 8])
    vones_d = dt_in("vones", [128, NBLK, 128])
    ident_d = dt_in("ident", [128, 128])
    y = nc.dram_tensor("y", [16 * 128, D], F32, kind="ExternalOutput").ap()

    dbg = nc.dram_tensor("dbg", [32, 128, D], F32, kind="ExternalOutput").ap() if DEBUG else None
    es = contextlib.ExitStack()
    with es:
        def sb(name, shape, dt):
            return es.enter_context(nc.sbuf_tensor("sb_" + name, list(shape), dt))

        h = sb("h", [128, 4, D], F32)
        Kr = [sb("Kr%d" % l, [128, 4, 1024], BF16) for l in range(2)]
        Vr = [sb("Vr%d" % l, [128, 8, 512], BF16) for l in range(2)]
        xnT = sb("xnT", [128, 8, 512], BF16)
        R = sb("R", [128, 16384], BF16)
        Wr = sb("Wr", [128, NS, 4096], BF16)
        biasb = sb("biasb", [128, 2, 8, 640], BF16)
        lng = sb("lng", [128, 2, 512], F32)
        lnb = sb("lnb", [128, 2, 512], F32)
        bsb = sb("bsb", [128, 2, 512], F32)
        wsTb = sb("wsTb", [128, 2, 4, 128], BF16)
        vonesb = sb("vonesb", [128, NBLK, 128], BF16)
        identb = sb("identb", [128, 128], BF16)
        gcols = sb("gcols", [128, 5, 8], F32)
        bgate = sb("bgate", [128, 2, 2, 8], F32)
        epsc = sb("epsc", [128, 1], F32)
        tinyc = sb("tinyc", [128, 1], F32)
        xn_tm = sb("xn_tm", [128, 2, D], BF16)
        sc = sb("sc", [128, 2, 640], F32)
        PT = sb("PT", [128, 3, 640], BF16)
        gv = sb("gv", [128, 4, 512], F32)
        gates = sb("gates", [128, 4, 512], F32)
        dtmp = sb("dtmp", [128, 512], F32)
        ss = sb("ss", [128, 4], F32)
        rstd = sb("rstd", [128, 4], F32)
        stats = sb("stats", [128, 4, 6], F32)
        mv = sb("mv", [128, 4, 2], F32)
        lrs = sb("lrs", [128, 4], F32)
        ps = es.enter_context(nc.psum_tensor("ps", [128, 8, 512], F32))
        psflat = ps[:].rearrange("p a b -> p (a b)")
        maskb = sc[:, 0, :]
        maskT = sc[:, 1, 0:128]
        junk = sc[:].rearrange("p a b -> p (a b)")[:, 0:1024]
        JUNK = [("sc", 0), ("sc", 1)]
        sg = gv[:, 0:2, :]
        gfin = gv[:, 2:4, :].rearrange("p a b -> p (a b)")

        Qz = R[:, 0:4096].rearrange("p (c t) -> p c t", c=8)
        uT = R[:, 4096:6144].rearrange("p (c t) -> p c t", c=4)
        vln = R[:, 6144:8192].rearrange("p (c t) -> p c t", c=4)
        attT = R[:, 8192:10240].rearrange("p (c t) -> p c t", c=4)
        sguT = R[:, 10240:12288].rearrange("p (c t) -> p c t", c=4)
        mrgT = R[:, 12288:16384].rearrange("p (c t) -> p c t", c=8)
        actT = R[:, 0:NFC * 512].rearrange("p (c t) -> p c t", c=NFC)
        RQ, RU, RV, RA, RS_, RM = "R_q", "R_u", "R_v", "R_a", "R_s", "R_m"
        R_ALL = [RQ, RU, RV, RA, RS_, RM]

        eng_sems = {e: es.enter_context(nc.semaphore("s_" + e)) for e in ENGS}
        dma_keys = ["c0", "c1", "c2"] + ["w%d" % i for i in range(NS)] + ["x%d" % i for i in range(4)] + \
                   ["o%d" % i for i in range(4)] + ["dbg"]
        dma_sems = {k: es.enter_context(nc.semaphore("d_" + k)) for k in dma_keys}
        block = es.enter_context(nc.Block())
        S = Sched()

        def pk(b):
            return [("ps", b)]

        S.op("sp", lambda e: e.dma_start(out=gcols[:], in_=gcols_d), writes=["gcols"], dma_sem="c0")
        S.op("sp", lambda e: e.dma_start(out=lng[:], in_=lng_d), writes=["lng"], dma_sem="c0")
        S.op("sp", lambda e: e.dma_start(out=lnb[:], in_=lnb_d), writes=["lnb"], dma_sem="c0")
        S.op("sp", lambda e: e.dma_start(out=bsb[:], in_=bsb_d), writes=["bsb"], dma_sem="c0")
        S.op("sp", lambda e: e.dma_start(out=bgate[:], in_=bgate_d), writes=["bgate"], dma_sem="c0")
        S.op("sp", lambda e: e.dma_start(out=maskT, in_=maskT_d), writes=[("sc", 1)], dma_sem="c0")
        S.op("sp", lambda e: e.dma_start(out=maskb, in_=maskb_d), writes=[("sc", 0)], dma_sem="c0")
        wsT_stage = gates[:].rearrange("p a b -> p (a b)")[:, 0:1024].rearrange("p (l g t) -> p l g t", l=2, g=4)
        S.op("sp", lambda e: e.dma_start(out=wsT_stage, in_=wsT_d), writes=[("gates", 0), ("gates", 1)], dma_sem="c0")
        S.dma_group_finalize("c0")
        S.op("pool", lambda e: e.dma_start(out=identb[:], in_=ident_d), writes=["identb"], dma_sem="c1")
        S.op("pool", lambda e: e.dma_start(out=vonesb[:], in_=vones_d), writes=["vonesb"], dma_sem="c1")
        S.dma_group_finalize("c1")
        S.op("dve", lambda e: e.memset(epsc[:], EPS), writes=["epsc"])
        S.op("dve", lambda e: e.memset(tinyc[:], 1e-20), writes=["tinyc"])
        S.op("dve", lambda e: e.tensor_tensor(out=wsTb[:].rearrange("p l g t -> p (l g) t"),
                                              in0=wsT_stage.rearrange("p l g t -> p (l g) t"),
                                              in1=maskT.unsqueeze(1).to_broadcast([128, 8, 128]), op=ALU.mult),
             reads=[("gates", 0), ("gates", 1), ("sc", 1)], writes=["wsTb"])
        Rf = R[:].bitcast(F32)
        bstage = Rf[:, 0:5120].rearrange("p (h c) -> p h c", h=8)
        for l in range(2):
            S.op("sp", (lambda l: lambda e: e.dma_start(out=bstage, in_=biasg_d[l]))(l), writes=R_ALL, dma_sem="c2")
            S.op("dve", (lambda l: lambda e: e.tensor_tensor(out=biasb[:, l, :, :], in0=bstage,
                                                             in1=maskb.unsqueeze(1).to_broadcast([128, 8, 640]),
                                                             op=ALU.add))(l),
                 reads=R_ALL + [("sc", 0)], writes=["biasb"])

        def units_for(l, kvonly):
            u = []

            def win(c0, n=512):
                return w_in[l][:, c0:c0 + n].rearrange("(c p) f -> p c f", p=128)

            def slot_view(kc, n):
                return (kc, n)

            if kvonly:
                u.append(("k", [((8, 512), win(512))]))
                u.append(("v", [((8, 512), win(1024))]))
                return u
            u.append(("vs", [((8, 512), win(2048))]))
            u.append(("u", [((8, 512), win(1536))]))
            u.append(("q", [((8, 512), win(0))]))
            u.append(("k", [((8, 512), win(512))]))
            u.append(("v", [((8, 512), win(1024))]))
            for hf in range(2):
                u.append(("br%d" % hf, [((4, 512, 0), w_bra[l][:, hf * 512:(hf + 1) * 512].rearrange("(c p) f -> p c f", p=128)),
                                        ((4, 512, 4), w_brs[l][:, hf * 512:(hf + 1) * 512].rearrange("(c p) f -> p c f", p=128))]))
                u.append(("g0%d" % hf, [((8, 512), win(2560 + hf * 512))]))
                u.append(("g1%d" % hf, [((8, 512), win(3584 + hf * 512))]))
            for hf in range(2):
                u.append(("wo%d" % hf, [((8, 512), w_out[l][:, hf * 512:(hf + 1) * 512].rearrange("(c p) f -> p c f", p=128))]))
            for i in range(6):
                n = 512 if i < 5 else 256
                u.append(("fg%d" % i, [((8, n), w_fi[l][:, i * 512:i * 512 + n].rearrange("(c p) f -> p c f", p=128))]))
                u.append(("fu%d" % i, [((8, n), w_fi[l][:, DFF + i * 512:DFF + i * 512 + n].rearrange("(c p) f -> p c f", p=128))]))
            for hf in range(2):
                for (c0, ncx) in ((0, 8), (8, 8), (16, 6)):
                    u.append(("fo%d_%d" % (hf, c0), [((ncx, 512), w_fo[l][c0 * 128:(c0 + ncx) * 128, hf * 512:(hf + 1) * 512]
                                                       .rearrange("(c p) f -> p c f", p=128))]))
            return u

        passes = []
        for s in range(n_sb):
            if s == 0:
                passes.append((0, s, True))
            else:
                passes.append((0, s, False))
                if n_layers > 1:
                    passes.append((1, s, s == 1))
        all_units = []
        for (l, s, kvonly) in passes:
            for (kind, dmas) in units_for(l, kvonly):
                all_units.append((l, s, kind, dmas))
        wstate = {"next_issue": 0, "next_use": 0}

        def issue_unit(n):
            l, s, kind, dmas = all_units[n]
            slot = n % NS
            for (spec, src) in dmas:
                if len(spec) == 2:
                    kc, ncol = spec
                    k0 = 0
                else:
                    kc, ncol, k0 = spec
                dst = Wr[:, slot, :].rearrange("p (c f) -> p c f", c=8)[:, k0:k0 + kc, 0:ncol] if ncol == 512 else \
                    Wr[:, slot, 0:8 * ncol].rearrange("p (c f) -> p c f", c=8)[:, k0:k0 + kc, :]
                S.op("pool", (lambda dst, src: lambda e: e.dma_start(out=dst, in_=src))(dst, src),
                     writes=[("w", slot, 0), ("w", slot, 1)], dma_sem="w%d" % slot)
            S.dma_group_finalize("w%d" % slot)

        def w_prefetch():
            while wstate["next_issue"] < len(all_units) and wstate["next_issue"] < wstate["next_use"] + NS:
                issue_unit(wstate["next_issue"])
                wstate["next_issue"] += 1

        def w_acquire(l, s, kind):
            n = wstate["next_use"]
            ul, us, ukind, dmas = all_units[n]
            assert (ul, us, ukind) == (l, s, kind), ((ul, us, ukind), (l, s, kind))
            assert wstate["next_issue"] > n
            wstate["next_use"] += 1
            slot = n % NS
            ncol = dmas[0][0][1]
            view = Wr[:, slot, 0:8 * ncol].rearrange("p (c f) -> p c f", c=8)
            return view, [("w", slot, 0), ("w", slot, 1)]

        def w_release():
            w_prefetch()

        w_prefetch()

        def blk_slot(b):
            return b % 8

        def load_x(s):
            for j in range(4):
                b = 4 * s + j
                S.op("sp", (lambda b, j: lambda e: e.dma_start(out=h[:, j, :], in_=xs[b * 128:(b + 1) * 128, :]))(b, j),
                     writes=[("h", j)], dma_sem="x%d" % j)

        tp_bank = [6]

        def norm_stats(j):
            S.op("act", lambda e: e.activation(out=junk, in_=h[:, j, :], func=AF.Square, accum_out=ss[:, j:j + 1]),
                 reads=[("h", j)], writes=JUNK + [("ss", j)])
            S.op("act", lambda e: e.activation(out=rstd[:, j:j + 1], in_=ss[:, j:j + 1], func=AF.Sqrt, scale=1.0 / D, bias=epsc[:]),
                 reads=[("ss", j), "epsc"], writes=[("rstd", j)])
            S.op("dve", lambda e: e.reciprocal(out=rstd[:, j:j + 1], in_=rstd[:, j:j + 1]), reads=[("rstd", j)], writes=[("rstd", j)])

        def stage_norm(gi):
            for j in range(4):
                norm_stats(j)
            banks = []

            def scale_and_transpose(j):
                xb = j % 2
                if j % 2 == 0:
                    S.op("dve", lambda e: e.tensor_scalar(out=xn_tm[:, xb, :], in0=h[:, j, :], scalar1=rstd[:, j:j + 1],
                                                          scalar2=None, op0=ALU.mult),
                         reads=[("h", j), ("rstd", j)], writes=[("xn_tm", xb)])
                else:
                    S.op("act", lambda e: e.activation(out=xn_tm[:, xb, :], in_=h[:, j, :], func=AF.Copy, scale=rstd[:, j:j + 1]),
                         reads=[("h", j), ("rstd", j)], writes=[("xn_tm", xb)])
                bank = tp_bank[0]
                tp_bank[0] = 6 if bank == 7 else 7
                pT = ps[:, bank, :].bitcast(BF16).rearrange("p (c t) -> p c t", c=8)
                for c in range(8):
                    S.op("pe", (lambda c, pT: lambda e: e.transpose(out=pT[:, c, :], in_=xn_tm[:, xb, c * 128:(c + 1) * 128],
                                                                     identity=identb[:]))(c, pT),
                         reads=[("xn_tm", xb), "identb"], writes=pk(bank))
                banks.append((bank, pT))

            def evac(j):
                bank, pT = banks[j]
                S.op("dve", lambda e: e.tensor_tensor(
                    out=xnT[:, :, j * 128:(j + 1) * 128], in0=pT,
                    in1=gcols[:, gi, :].unsqueeze(2).to_broadcast([128, 8, 128]), op=ALU.mult),
                    reads=pk(bank) + ["gcols"], writes=[("xnT", j)])

            scale_and_transpose(0)
            for j in range(1, 4):
                scale_and_transpose(j)
                evac(j - 1)
            evac(3)

        XNT_ALL = [("xnT", j) for j in range(4)]
        acc_bank = [0]

        def next_bank(lo=0, n=4):
            b = lo + acc_bank[0] % n
            acc_bank[0] += 1
            return b

        def proj_fm(l, s, kind, evac):
            wv, wkey = w_acquire(l, s, kind)
            for fc in range(4):
                bank = next_bank()
                for k in range(8):
                    S.op("pe", (lambda fc, k, bank: lambda e: e.matmul(ps[:, bank, :], lhsT=wv[:, k, fc * 128:(fc + 1) * 128],
                                                                       rhs=xnT[:, k, :], start=(k == 0), stop=(k == 7)))(fc, k, bank),
                         reads=wkey + XNT_ALL, writes=pk(bank))
                evac(fc, bank)
            w_release()

        def proj_tm(l, s, kind, evac):
            wv, wkey = w_acquire(l, s, kind)
            for j in range(4):
                bank = next_bank()
                for k in range(8):
                    S.op("pe", (lambda j, k, bank: lambda e: e.matmul(ps[:, bank, :], lhsT=xnT[:, k, j * 128:(j + 1) * 128],
                                                                      rhs=wv[:, k, :], start=(k == 0), stop=(k == 7)))(j, k, bank),
                         reads=wkey + [("xnT", j)], writes=pk(bank))
                evac(j, bank)
            w_release()

        def stage_proj(l, s, kvonly):
            koff = ((4 * s) % 8) * 128

            def ev_q(fc, bank):
                if fc == 0:
                    S.op("pool", lambda e: e.memset(Qz[64:128, 0:8:2, :], 0.0), writes=[RQ])
                    S.op("pool", lambda e: e.memset(Qz[0:64, 1:8:2, :], 0.0), writes=[RQ])
                S.op("act", lambda e: e.activation(out=Qz[0:64, 2 * fc, :], in_=ps[0:64, bank, :], func=AF.Copy, scale=0.125),
                     reads=pk(bank), writes=[RQ, ("Qh", 2 * fc)])
                S.op("act", lambda e: e.activation(out=Qz[64:128, 2 * fc + 1, :], in_=ps[64:128, bank, :], func=AF.Copy, scale=0.125),
                     reads=pk(bank), writes=[RQ, ("Qh", 2 * fc + 1)])

            def ev_k(fc, bank):
                S.op("dve", lambda e: e.tensor_copy(out=Kr[l][:, fc, koff:koff + 512], in_=ps[:, bank, :]),
                     reads=pk(bank), writes=[("Kr", l, (4 * s) % 8 // 4)])

            def ev_v(j, bank):
                slot = blk_slot(4 * s + j)
                S.op("dve", lambda e: e.tensor_copy(out=Vr[l][:, slot, :], in_=ps[:, bank, :]),
                     reads=pk(bank), writes=[("Vr", l, slot)])

            def ev_u(fc, bank):
                S.op("act", lambda e: e.activation(out=uT[:, fc, :], in_=ps[:, bank, :], func=AF.Gelu_apprx_tanh),
                     reads=pk(bank), writes=[RU])

            def ev_vs(j, bank):
                S.op("act", lambda e: e.activation(out=gv[:, j, :], in_=ps[:, bank, :], func=AF.Gelu_apprx_tanh),
                     reads=pk(bank), writes=[("gv", j)])
                S.op("dve", lambda e: e.bn_stats(out=stats[:, j, :], in_=gv[:, j, :]), reads=[("gv", j)], writes=[("stats", j)])
                S.op("dve", lambda e: e.bn_aggr(out=mv[:, j, :], in_=stats[:, j, :]), reads=[("stats", j)], writes=[("mv", j)])

            if kvonly:
                proj_fm(l, s, "k", ev_k)
                proj_tm(l, s, "v", ev_v)
                return
            proj_tm(l, s, "vs", ev_vs)
            proj_fm(l, s, "u", ev_u)
            S.op("act", lambda e: e.activation(out=lrs[:], in_=mv[:, :, 1], func=AF.Sqrt, scale=1.0, bias=epsc[:]),
                 reads=[("mv", j) for j in range(4)] + ["epsc"], writes=["lrs"])
            S.op("dve", lambda e: e.reciprocal(out=lrs[:], in_=lrs[:]), reads=["lrs"], writes=["lrs"])
            for j in range(4):
                S.op("dve", (lambda j: lambda e: e.tensor_scalar(out=gv[:, j, :], in0=gv[:, j, :], scalar1=mv[:, j, 0:1],
                                                                  scalar2=lrs[:, j:j + 1], op0=ALU.subtract, op1=ALU.mult))(j),
                     reads=[("gv", j), ("mv", j), "lrs"], writes=[("gv", j)])
                S.op("dve", (lambda j: lambda e: e.tensor_tensor(out=gv[:, j, :], in0=gv[:, j, :], in1=lng[:, l, :], op=ALU.mult))(j),
                     reads=[("gv", j), "lng"], writes=[("gv", j)])
                S.op("dve", (lambda j: lambda e: e.tensor_tensor(out=vln[:, j, :], in0=gv[:, j, :], in1=lnb[:, l, :], op=ALU.add))(j),
                     reads=[("gv", j), "lnb"], writes=[RV])
            proj_fm(l, s, "q", ev_q)
            proj_fm(l, s, "k", ev_k)
            proj_tm(l, s, "v", ev_v)

        sbuf_ctr = {"sA": 0, "sc": 0, "pt": 0, "dt": 0}

        def stage_attn(l, s):
            units = [(j, hh) for j in range(4) for hh in range(8)]
            st = {}

            def emit_qk(n):
                j, hh = units[n]
                p = hh // 2
                b = 4 * s + j
                kslots = [blk_slot(b - 4 + i) for i in range(5)]
                kreads = [("Kr", l, ks // 4) for ks in set(kslots)]
                sbi = sbuf_ctr["sA"] % 3
                sbuf_ctr["sA"] += 1
                base = sbi * 1024
                banks = [2 * sbi, 2 * sbi + 1]
                for i in range(5):
                    ks = kslots[i]
                    col = base + i * 128
                    S.op("pe", (lambda col, ks, p, hh, j: lambda e: e.matmul(
                        psflat[:, col:col + 128], lhsT=Kr[l][:, p, ks * 128:(ks + 1) * 128],
                        rhs=Qz[:, hh, j * 128:(j + 1) * 128], start=True, stop=True))(col, ks, p, hh, j),
                        reads=kreads + [RQ, ("Qh", hh)], writes=pk(col // 512))
                sci = sbuf_ctr["sc"] % 2
                sbuf_ctr["sc"] += 1
                S.op("dve", (lambda base, hh, sci: lambda e: e.tensor_tensor(
                    out=sc[:, sci, :], in0=psflat[:, base:base + 640], in1=biasb[:, l, hh, :], op=ALU.add))(base, hh, sci),
                    reads=pk(banks[0]) + pk(banks[1]) + ["biasb"], writes=[("sc", sci)])
                pi = sbuf_ctr["pt"] % 3
                sbuf_ctr["pt"] += 1
                S.op("act", (lambda sci, pi: lambda e: e.activation(out=PT[:, pi, :], in_=sc[:, sci, :], func=AF.Exp))(sci, pi),
                     reads=[("sc", sci)], writes=[("PT", pi)])
                st[n] = (kslots, pi)

            def emit_pv(n):
                j, hh = units[n]
                p, par = hh // 2, hh % 2
                b = 4 * s + j
                kslots, pi = st.pop(n)
                bank = 6 + p % 2
                for kind in ("num", "den"):
                    cc = par * 128 + (0 if kind == "num" else 256)
                    for i in range(5):
                        ks = kslots[i]
                        if kind == "num":
                            lhs = Vr[l][:, ks, p * 128:(p + 1) * 128]
                            rd = [("Vr", l, ks)]
                        else:
                            lhs = vonesb[:, b - 4 + i, :]
                            rd = ["vonesb"]
                        S.op("pe", (lambda lhs, pi, i, bank, cc: lambda e: e.matmul(
                            ps[:, bank, cc:cc + 128], lhsT=lhs,
                            rhs=PT[:, pi, i * 128:(i + 1) * 128], start=(i == 0), stop=(i == 4)))(lhs, pi, i, bank, cc),
                            reads=rd + [("PT", pi)], writes=pk(bank))
                if par != 1:
                    return None

                di = sbuf_ctr["dt"] % 2
                sbuf_ctr["dt"] += 1
                dsl = dtmp[:, di * 256:(di + 1) * 256]

                def norm_act():
                    S.op("act", (lambda bank, dsl: lambda e: e.activation(out=dsl, in_=ps[:, bank, 256:512], func=AF.Ln,
                                                                          bias=tinyc[:]))(bank, dsl),
                         reads=pk(bank) + ["tinyc"], writes=[("dtmp", di)])
                    S.op("act", (lambda dsl: lambda e: e.activation(out=dsl, in_=dsl, func=AF.Exp, scale=-1.0))(dsl),
                         reads=[("dtmp", di)], writes=[("dtmp", di)])

                def norm_dve():
                    for hf2 in range(2):
                        rs = slice(hf2 * 64, (hf2 + 1) * 64)
                        S.op("dve", (lambda j, p, bank, dsl, rs, hf2: lambda e: e.tensor_tensor(
                            out=attT[rs, p, j * 128:(j + 1) * 128], in0=ps[rs, bank, hf2 * 128:(hf2 + 1) * 128],
                            in1=dsl[rs, hf2 * 128:(hf2 + 1) * 128], op=ALU.mult))(j, p, bank, dsl, rs, hf2),
                            reads=pk(bank) + [("dtmp", di)], writes=[RA])
                return (norm_act, norm_dve)

            nu = len(units)
            pend_act, pend_dve = [], []
            for n in range(min(ATT_SKEW, nu)):
                emit_qk(n)
            for n in range(nu):
                if n + ATT_SKEW < nu:
                    emit_qk(n + ATT_SKEW)
                if n % 2 == 1 and pend_act:
                    fa, fd = pend_act.pop(0)
                    fa()
                    pend_dve.append(fd)
                if n % 2 == 0 and pend_dve:
                    pend_dve.pop(0)()
                f = emit_pv(n)
                if f is not None:
                    pend_act.append(f)
            while pend_dve:
                pend_dve.pop(0)()
            while pend_act:
                fa, fd = pend_act.pop(0)
                fa()
                fd()

        def stage_sgu(l, s):
            for j in range(4):
                bank = next_bank()
                for g in range(4):
                    S.op("pe", (lambda j, g, bank: lambda e: e.matmul(ps[:, bank, g * 128:(g + 1) * 128],
                                                                      lhsT=vln[:, j, g * 128:(g + 1) * 128],
                                                                      rhs=wsTb[:, l, g, :], start=True, stop=True))(j, g, bank),
                         reads=[RV, "wsTb"], writes=pk(bank))
                gb = gates[:, j % 2, :]
                S.op("dve", (lambda bank, gb: lambda e: e.tensor_tensor(out=gb, in0=ps[:, bank, :], in1=bsb[:, l, :], op=ALU.add))(bank, gb),
                     reads=pk(bank) + ["bsb"], writes=[("gates", j % 2)])
                S.op("dve", (lambda j, gb: lambda e: e.tensor_tensor(
                    out=sguT[:, :, j * 128:(j + 1) * 128], in0=gb.rearrange("p (c t) -> p c t", c=4),
                    in1=uT[:, :, j * 128:(j + 1) * 128], op=ALU.mult))(j, gb),
                    reads=[("gates", j % 2), RU], writes=[RS_])

        def stage_merge(l, s):
            for hf in range(2):
                wbr, kbr = w_acquire(l, s, "br%d" % hf)
                wg0, kg0 = w_acquire(l, s, "g0%d" % hf)
                wg1, kg1 = w_acquire(l, s, "g1%d" % hf)
                for f4 in range(4):
                    fc = hf * 4 + f4
                    bset = (fc % 2) * 4
                    bA, bB, bG0, bG1 = bset, bset + 1, bset + 2, bset + 3
                    cs = slice(f4 * 128, (f4 + 1) * 128)
                    for k in range(4):
                        S.op("pe", (lambda k, bA, cs, wbr: lambda e: e.matmul(ps[:, bA, :], lhsT=wbr[:, k, cs], rhs=attT[:, k, :],
                                                                              start=(k == 0), stop=(k == 3)))(k, bA, cs, wbr),
                             reads=kbr + [RA], writes=pk(bA))
                    for k in range(4):
                        S.op("pe", (lambda k, bB, cs, wbr: lambda e: e.matmul(ps[:, bB, :], lhsT=wbr[:, 4 + k, cs], rhs=sguT[:, k, :],
                                                                              start=(k == 0), stop=(k == 3)))(k, bB, cs, wbr),
                             reads=kbr + [RS_], writes=pk(bB))
                    for (wg, kg, bG, gi) in ((wg0, kg0, bG0, 0), (wg1, kg1, bG1, 1)):
                        for k in range(8):
                            S.op("pe", (lambda k, bG, cs, wg: lambda e: e.matmul(ps[:, bG, :], lhsT=wg[:, k, cs], rhs=xnT[:, k, :],
                                                                                 start=(k == 0), stop=(k == 7)))(k, bG, cs, wg),
                                 reads=kg + XNT_ALL, writes=pk(bG))
                        gidx = (fc % 2) * 2 + gi
                        S.op("act", (lambda bG, gidx, gi, fc: lambda e: e.activation(
                            out=gates[:, gidx, :], in_=ps[:, bG, :], func=AF.Sigmoid, bias=bgate[:, l, gi, fc:fc + 1]))(bG, gidx, gi, fc),
                            reads=pk(bG) + ["bgate"], writes=[("gates", gidx)])
                    g0i, g1i = (fc % 2) * 2, (fc % 2) * 2 + 1
                    S.op("dve", (lambda bA, g0i: lambda e: e.tensor_tensor(out=gates[:, g0i, :], in0=ps[:, bA, :], in1=gates[:, g0i, :],
                                                                           op=ALU.mult))(bA, g0i),
                         reads=pk(bA) + [("gates", g0i)], writes=[("gates", g0i)])
                    S.op("dve", (lambda bB, g1i: lambda e: e.tensor_tensor(out=gates[:, g1i, :], in0=ps[:, bB, :], in1=gates[:, g1i, :],
                                                                           op=ALU.mult))(bB, g1i),
                         reads=pk(bB) + [("gates", g1i)], writes=[("gates", g1i)])
                    S.op("dve", (lambda fc, g0i, g1i: lambda e: e.tensor_tensor(out=mrgT[:, fc, :], in0=gates[:, g0i, :],
                                                                                in1=gates[:, g1i, :], op=ALU.add))(fc, g0i, g1i),
                         reads=[("gates", g0i), ("gates", g1i)], writes=[RM])
                w_release()
                w_release()
                w_release()

        def stage_wout(l, s):
            for hf in range(2):
                wv, wkey = w_acquire(l, s, "wo%d" % hf)
                for j in range(4):
                    bank = next_bank()
                    for k in range(8):
                        S.op("pe", (lambda j, k, bank, wv: lambda e: e.matmul(ps[:, bank, :], lhsT=mrgT[:, k, j * 128:(j + 1) * 128],
                                                                              rhs=wv[:, k, :], start=(k == 0), stop=(k == 7)))(j, k, bank, wv),
                             reads=wkey + [RM], writes=pk(bank))
                    S.op("dve", (lambda j, bank, hf: lambda e: e.tensor_tensor(out=h[:, j, hf * 512:(hf + 1) * 512], in0=ps[:, bank, :],
                                                                               in1=h[:, j, hf * 512:(hf + 1) * 512], op=ALU.add))(j, bank, hf),
                         reads=pk(bank) + [("h", j)], writes=[("h", j)])
                w_release()

        def stage_ffn_in(l, s):
            for i in range(6):
                wg, kg = w_acquire(l, s, "fg%d" % i)
                wu, ku = w_acquire(l, s, "fu%d" % i)
                nch = 4 if i < 5 else 2
                for c4 in range(nch):
                    c = i * 4 + c4
                    bset = (c % 4) * 2
                    bG, bU = bset, bset + 1
                    cs = slice(c4 * 128, (c4 + 1) * 128)
                    for (wv, wk, bank) in ((wg, kg, bG), (wu, ku, bU)):
                        for k in range(8):
                            S.op("pe", (lambda k, bank, cs, wv: lambda e: e.matmul(ps[:, bank, :], lhsT=wv[:, k, cs], rhs=xnT[:, k, :],
                                                                                   start=(k == 0), stop=(k == 7)))(k, bank, cs, wv),
                                 reads=wk + XNT_ALL, writes=pk(bank))
                    si = c % 2
                    S.op("act", (lambda bG, si: lambda e: e.activation(out=sg[:, si, :], in_=ps[:, bG, :], func=AF.Silu))(bG, si),
                         reads=pk(bG), writes=[("gv", si)])
                    S.op("dve", (lambda bU, si, c: lambda e: e.tensor_tensor(out=actT[:, c, :], in0=ps[:, bU, :], in1=sg[:, si, :],
                                                                             op=ALU.mult))(bU, si, c),
                         reads=pk(bU) + [("gv", si)], writes=R_ALL)
                w_release()
                w_release()

        def stage_ffn_out(l, s):
            for hf in range(2):
                ws = []
                for (c0, ncx) in ((0, 8), (8, 8), (16, 6)):
                    wv, wkey = w_acquire(l, s, "fo%d_%d" % (hf, c0))
                    ws.append((c0, ncx, wv, wkey))
                for j in range(4):
                    bank = next_bank()
                    for (c0, ncx, wv, wkey) in ws:
                        for cc in range(ncx):
                            c = c0 + cc
                            S.op("pe", (lambda j, c, cc, bank, wv: lambda e: e.matmul(
                                ps[:, bank, :], lhsT=actT[:, c, j * 128:(j + 1) * 128], rhs=wv[:, cc, :],
                                start=(c == 0), stop=(c == NFC - 1)))(j, c, cc, bank, wv),
                                reads=wkey + R_ALL, writes=pk(bank))
                    S.op("dve", (lambda j, bank, hf: lambda e: e.tensor_tensor(out=h[:, j, hf * 512:(hf + 1) * 512], in0=ps[:, bank, :],
                                                                               in1=h[:, j, hf * 512:(hf + 1) * 512], op=ALU.add))(j, bank, hf),
                         reads=pk(bank) + [("h", j)], writes=[("h", j)])
                w_release()
                w_release()
                w_release()

        def stage_final(s):
            S.op("sp", lambda e: e.dma_start(out=gfin, in_=gfin_d), writes=[("gv", 2), ("gv", 3)], dma_sem="c0")
            S.dma_group_finalize("c0")
            yst = R[:].bitcast(F32)[:, 0:4 * D].rearrange("p (j f) -> p j f", j=4)
            for j in range(4):
                norm_stats(j)
                S.op("dve", (lambda j: lambda e: e.scalar_tensor_tensor(
                    out=yst[:, j, :], in0=h[:, j, :], scalar=rstd[:, j:j + 1], in1=gfin, op0=ALU.mult, op1=ALU.mult))(j),
                    reads=[("h", j), ("rstd", j), ("gv", 2), ("gv", 3)], writes=(R_ALL if j == 0 else []) + [("yst", j)])
                ob = (4 * s + j) - 8
                S.op("sp", (lambda j, ob: lambda e: e.dma_start(out=y[ob * 128:(ob + 1) * 128, :], in_=yst[:, j, :]))(j, ob),
                     reads=R_ALL + [("yst", j)], dma_sem="o%d" % j)

        gfin_d = dt_in("gfin", [128, D])

        dstage = sb("dstage", [128, D], F32) if DEBUG else None
        dctr = [0]

        def dump(name, ap, reads, n=D):
            if not DEBUG or name not in dumps:
                return
            idx = dctr[0]
            dctr[0] += 1
            _LAST_DUMP_ORDER.append(name)
            S.op("dve", lambda e: e.tensor_copy(out=dstage[:, 0:n], in_=ap), reads=reads, writes=["dstage"])
            S.op("sp", lambda e: e.dma_start(out=dbg[idx][:, 0:n], in_=dstage[:, 0:n]), reads=["dstage"], dma_sem="dbg")

        def chk(tag):
            if stop_after == tag:
                raise _Stop()

        def run_layer(l, s, kvonly):
            T = "L%d_S%d_" % (l, s)
            stage_norm(2 * l)
            dump(T + "xnT0", xnT[:, 0, :], XNT_ALL, 512)
            dump(T + "xnT7", xnT[:, 7, :], XNT_ALL, 512)
            chk(T + "norm")
            stage_proj(l, s, kvonly)
            dump(T + "K0", Kr[l][:, 0, :], [("Kr", l, 0), ("Kr", l, 1)], 1024)
            dump(T + "V0", Vr[l][:, blk_slot(4 * s), :], [("Vr", l, blk_slot(4 * s))], 512)
            if kvonly:
                chk(T + "proj")
                return
            dump(T + "q0", Qz[:, 0, :], [RQ, ("Qh", 0)], 512)
            dump(T + "u0", uT[:, 0, :], [RU], 512)
            dump(T + "vln0", vln[:, 0, :], [RV], 512)
            chk(T + "proj")
            stage_attn(l, s)
            dump(T + "att0", attT[:, 0, :], [RA], 512)
            dump(T + "att3", attT[:, 3, :], [RA], 512)
            chk(T + "attn")
            stage_sgu(l, s)
            dump(T + "sgu0", sguT[:, 0, :], [RS_], 512)
            chk(T + "sgu")
            stage_merge(l, s)
            for fcx in range(8):
                dump(T + "mrg%d" % fcx, mrgT[:, fcx, :], [RM], 512)
            chk(T + "merge")
            stage_wout(l, s)
            dump(T + "h0mid", h[:, 0, :], [("h", 0)])
            chk(T + "wout")
            stage_norm(2 * l + 1)
            stage_ffn_in(l, s)
            dump(T + "act0", actT[:, 0, :], R_ALL, 512)
            dump(T + "act21", actT[:, 21, :], R_ALL, 512)
            chk(T + "ffn_in")
            stage_ffn_out(l, s)
            dump(T + "h0", h[:, 0, :], [("h", 0)])
            dump(T + "h3", h[:, 3, :], [("h", 3)])
            chk(T + "ffn_out")

        try:
            for s in range(n_sb):
                load_x(s)
                run_layer(0, s, s == 0)
                if s >= 1 and n_layers > 1:
                    run_layer(1, s, s == 1)
                if s >= 2:
                    stage_final(s)
            assert wstate["next_use"] == len(all_units), (wstate, len(all_units))
        except _Stop:
            pass

        S.final_waits["sp"] = [("o%d" % j, S.dma_counts.get("o%d" % j, 0)) for j in range(4)]
        S.final_waits["pool"] = [(k, v) for k, v in S.dma_counts.items() if k.startswith("w") or k == "c1"]
        S.final_waits["sp"] += [(k, v) for k, v in S.dma_counts.items() if k.startswith("x") or k in ("c0", "c2")]
        if DEBUG:
            S.final_waits["sp"].append(("dbg", S.dma_counts.get("dbg", 0)))
        S.emit(block, eng_sems, dma_sems)
    return nc


def _host_consts(att_rel_bias, sgu_norm_gain, sgu_norm_bias, sgu_w, sgu_b, b_gate, norm_mix, norm_ffn, norm_final):
    f32 = np.float32
    gl = [norm_mix[0], norm_ffn[0], norm_mix[1], norm_ffn[1], norm_final]
    gcols = np.stack([np.asarray(g, f32).reshape(8, 128).T for g in gl], axis=1)
    lng = np.broadcast_to(np.asarray(sgu_norm_gain, f32)[None], (128, 2, 512)).copy()
    lnb = np.broadcast_to(np.asarray(sgu_norm_bias, f32)[None], (128, 2, 512)).copy()
    bsb = np.broadcast_to(np.asarray(sgu_b, f32).reshape(2, 512)[None], (128, 2, 512)).copy()
    wsT = np.ascontiguousarray(np.asarray(sgu_w, f32).transpose(3, 0, 1, 2))
    maskT = (np.arange(128)[:, None] <= np.arange(128)[None, :]).astype(f32)
    ki = np.arange(128)[:, None, None]
    i5 = np.arange(5)[None, :, None]
    qi = np.arange(128)[None, None, :]
    rel = np.clip((4 - i5) * 128 + qi - ki, -128, 128) + 128
    biasg = np.asarray(att_rel_bias, f32)[:, :, rel]
    biasg = np.ascontiguousarray(biasg.transpose(0, 2, 1, 3, 4)).reshape(2, 128, 8, 640)
    dchunk = (8 - 2 * i5) + (qi // 64) - (ki // 64)
    valid = (dchunk >= 0) & (dchunk <= 8)
    maskb = np.where(valid, 0.0, NEG).astype(f32).reshape(128, 640)
    bgate = np.ascontiguousarray(np.asarray(b_gate, f32).reshape(2, 2, 8, 128).transpose(3, 0, 1, 2))
    gfin = np.broadcast_to(np.asarray(norm_final, f32)[None], (128, D)).copy()
    ident = np.eye(128, dtype=f32)
    return dict(gcols=gcols, lng=lng, lnb=lnb, bsb=bsb, wsT=wsT, maskT=maskT, biasg=biasg, maskb=maskb,
                bgate=bgate, gfin=gfin, ident=ident)


def _make_in_maps(x, shared):
    f32 = np.float32
    in_maps = []
    for c in range(8):
        b, half = c // 2, c % 2
        xs = np.zeros((NBLK * 128, D), f32)
        vones = np.ones((128, NBLK, 128), f32)
        if half == 0:
            xs[1024:] = x[b, 0:2048]
            vones[:, 0:8, :] = 0.0
        else:
            xs[:] = x[b, 1024:4096]
        m = dict(shared)
        m["xs"] = xs
        m["vones"] = vones
        in_maps.append(m)
    return in_maps


_PROGRAM = {}


def kernel(x, norm_mix, w_in, att_rel_bias, sgu_norm_gain, sgu_norm_bias, sgu_w, sgu_b,
           w_br_att, w_br_sgu, b_gate, w_out, norm_ffn, w_ffn_in, w_ffn_out, norm_final):
    f32 = np.float32
    x = np.asarray(x, f32)
    consts = _host_consts(att_rel_bias, sgu_norm_gain, sgu_norm_bias, sgu_w, sgu_b, b_gate, norm_mix, norm_ffn, norm_final)
    shared = dict(w_in=np.ascontiguousarray(w_in, f32), w_br_att=np.ascontiguousarray(w_br_att, f32),
                  w_br_sgu=np.ascontiguousarray(w_br_sgu, f32), w_out=np.ascontiguousarray(w_out, f32),
                  w_ffn_in=np.ascontiguousarray(w_ffn_in, f32), w_ffn_out=np.ascontiguousarray(w_ffn_out, f32))
    shared.update(consts)
    in_maps = _make_in_maps(x, shared)
    if "nc" not in _PROGRAM:
        _PROGRAM["nc"] = build_program()
    res = run_bass_kernel_spmd(_PROGRAM["nc"], in_maps, core_ids=list(range(8)))
    out = np.empty((4, 4096, D), f32)
    for c in range(8):
        b, half = c // 2, c % 2
        out[b, half * 2048:(half + 1) * 2048] = res.results[c]["y"]
    return out
```

```python
import contextlib
import numpy as np
import concourse.bass as bass
import concourse.mybir as mybir
from concourse.bass_utils import run_bass_kernel_spmd

F32 = mybir.dt.float32
BF16 = mybir.dt.bfloat16
AF = mybir.ActivationFunctionType
ALU = mybir.AluOpType

D = 1024
DFF = 2816
NFC = 22
NBLK = 24
NSB = 6
NS = 6
EPS = 1e-6
NEG = -30000.0
ATT_SKEW = 2

ENGS = ("pe", "act", "dve", "pool", "sp")


class Op:
    __slots__ = ("eng", "idx", "fn", "waits", "signal", "ev_sem", "ev_val", "is_dma", "sigidx")

    def __init__(self, eng, idx, fn, is_dma):
        self.eng = eng
        self.idx = idx
        self.fn = fn
        self.waits = []
        self.signal = False
        self.is_dma = is_dma
        self.ev_sem = None
        self.ev_val = None
        self.sigidx = None


class Sched:
    def __init__(self):
        self.ops = {e: [] for e in ENGS}
        self.last_w = {}
        self.readers = {}
        self.dma_counts = {}
        self.waited = {e: {} for e in ENGS}
        self.final_waits = {}
        self.group_ops = {}

    def _add_dep(self, op, dep):
        if dep is None or dep is op:
            return
        if dep.is_dma:
            key = ("dma", dep.ev_sem)
            prev = self.waited[op.eng].get(key, 0)
            if dep.ev_val <= prev:
                return
            self.waited[op.eng][key] = dep.ev_val
            op.waits.append((dep.ev_sem, dep.ev_val))
            return
        if dep.eng == op.eng:
            return
        prev = self.waited[op.eng].get(dep.eng, -1)
        if dep.idx <= prev:
            return
        self.waited[op.eng][dep.eng] = dep.idx
        dep.signal = True
        op.waits.append(dep)

    def op(self, eng, fn, reads=(), writes=(), dma_sem=None):
        is_dma = dma_sem is not None
        excl = [b for b in reads if isinstance(b, tuple) and b[0] == "ps"]
        if excl:
            reads = [b for b in reads if not (isinstance(b, tuple) and b[0] == "ps")]
            writes = list(writes) + excl
        o = Op(eng, len(self.ops[eng]), fn, is_dma)
        if is_dma:
            c = self.dma_counts.get(dma_sem, 0) + 16
            self.dma_counts[dma_sem] = c
            o.ev_sem = dma_sem
            o.ev_val = c
            self.group_ops.setdefault(dma_sem, []).append(o)
        for b in reads:
            w = self.last_w.get(b)
            if w is not None:
                if (not w.is_dma) and w.eng == eng and not is_dma:
                    if eng != "pe" and o.idx - w.idx <= 2:
                        w.signal = True
                        if w not in o.waits:
                            o.waits.append(w)
                else:
                    self._add_dep(o, w)
        for b in writes:
            w = self.last_w.get(b)
            if w is not None:
                self._add_dep(o, w)
            for r in self.readers.get(b, {}).values():
                self._add_dep(o, r)
        for b in reads:
            self.readers.setdefault(b, {})[(eng, o.ev_sem) if is_dma else eng] = o
        for b in writes:
            self.last_w[b] = o
            self.readers[b] = {}
        self.ops[eng].append(o)
        return o

    def dma_group_finalize(self, key):
        tot = self.dma_counts.get(key, 0)
        for o in self.group_ops.get(key, []):
            o.ev_val = tot
        self.group_ops[key] = []

    def emit(self, block, sems, dma_sems):
        for e in ENGS:
            n = 0
            for o in self.ops[e]:
                if (not o.is_dma) and o.signal:
                    n += 1
                    o.sigidx = n

        def run(engname, engine):
            for o in self.ops[engname]:
                for d in o.waits:
                    if isinstance(d, tuple):
                        engine.wait_ge(dma_sems[d[0]], d[1])
                    else:
                        engine.wait_ge(sems[d.eng], d.sigidx)
                ins = o.fn(engine)
                if o.is_dma:
                    ins.then_inc(dma_sems[o.ev_sem], 16)
                elif o.signal:
                    ins.then_inc(sems[o.eng], 1)
            for key, cnt in self.final_waits.get(engname, []):
                engine.wait_ge(dma_sems[key], cnt)

        block.tensor(lambda e: run("pe", e))
        block.scalar(lambda e: run("act", e))
        block.vector(lambda e: run("dve", e))
        block.gpsimd(lambda e: run("pool", e))
        block.sync(lambda e: run("sp", e))


class _Stop(Exception):
    pass


_LAST_DUMP_ORDER = []


def build_program(n_sb=NSB, n_layers=2, stop_after=None, dumps=None):
    DEBUG = dumps is not None
    del _LAST_DUMP_ORDER[:]
    nc = bass.Bass("TRN2", target_bir_lowering=False)
    dt_in = lambda name, shape: nc.dram_tensor(name, list(shape), F32, kind="ExternalInput").ap()
    xs = dt_in("xs", [NBLK * 128, D])
    w_in = dt_in("w_in", [2, D, 4608])
    w_bra = dt_in("w_br_att", [2, 512, D])
    w_brs = dt_in("w_br_sgu", [2, 512, D])
    w_out = dt_in("w_out", [2, D, D])
    w_fi = dt_in("w_ffn_in", [2, D, 2 * DFF])
    w_fo = dt_in("w_ffn_out", [2, DFF, D])
    gcols_d = dt_in("gcols", [128, 5, 8])
    lng_d = dt_in("lng", [128, 2, 512])
    lnb_d = dt_in("lnb", [128, 2, 512])
    bsb_d = dt_in("bsb", [128, 2, 512])
    wsT_d = dt_in("wsT", [128, 2, 4, 128])
    maskT_d = dt_in("maskT", [128, 128])
    biasg_d = dt_in("biasg", [2, 128, 8, 640])
    maskb_d = dt_in("maskb", [128, 640])
    bgate_d = dt_in("bgate", [128, 2, 2, 8])
    vones_d = dt_in("vones", [128, NBLK, 128])
    ident_d = dt_in("ident", [128, 128])
    y = nc.dram_tensor("y", [16 * 128, D], F32, kind="ExternalOutput").ap()

    dbg = nc.dram_tensor("dbg", [32, 128, D], F32, kind="ExternalOutput").ap() if DEBUG else None
    es = contextlib.ExitStack()
    with es:
        def sb(name, shape, dt):
            return es.enter_context(nc.sbuf_tensor("sb_" + name, list(shape), dt))

        h = sb("h", [128, 4, D], F32)
        Kr = [sb("Kr%d" % l, [128, 4, 1024], BF16) for l in range(2)]
        Vr = [sb("Vr%d" % l, [128, 8, 512], BF16) for l in range(2)]
        xnT = sb("xnT", [128, 8, 512], BF16)
        R = sb("R", [128, 16384], BF16)
        Wr = sb("Wr", [128, NS, 4096], BF16)
        biasb = sb("biasb", [128, 2, 8, 640], BF16)
        lng = sb("lng", [128, 2, 512], F32)
        lnb = sb("lnb", [128, 2, 512], F32)
        bsb = sb("bsb", [128, 2, 512], F32)
        wsTb = sb("wsTb", [128, 2, 4, 128], BF16)
        vonesb = sb("vonesb", [128, NBLK, 128], BF16)
        identb = sb("identb", [128, 128], BF16)
        gcols = sb("gcols", [128, 5, 8], F32)
        bgate = sb("bgate", [128, 2, 2, 8], F32)
        epsc = sb("epsc", [128, 1], F32)
        tinyc = sb("tinyc", [128, 1], F32)
        xn_tm = sb("xn_tm", [128, 2, D], BF16)
        sc = sb("sc", [128, 2, 640], F32)
        PT = sb("PT", [128, 3, 640], BF16)
        gv = sb("gv", [128, 4, 512], F32)
        gates = sb("gates", [128, 4, 512], F32)
        dtmp = sb("dtmp", [128, 512], F32)
        ss = sb("ss", [128, 4], F32)
        rstd = sb("rstd", [128, 4], F32)
        stats = sb("stats", [128, 4, 6], F32)
        mv = sb("mv", [128, 4, 2], F32)
        lrs = sb("lrs", [128, 4], F32)
        ps = es.enter_context(nc.psum_tensor("ps", [128, 8, 512], F32))
        psflat = ps[:].rearrange("p a b -> p (a b)")
        maskb = sc[:, 0, :]
        maskT = sc[:, 1, 0:128]
        junk = sc[:].rearrange("p a b -> p (a b)")[:, 0:1024]
        JUNK = [("sc", 0), ("sc", 1)]
        sg = gv[:, 0:2, :]
        gfin = gv[:, 2:4, :].rearrange("p a b -> p (a b)")

        Qz = R[:, 0:4096].rearrange("p (c t) -> p c t", c=8)
        uT = R[:, 4096:6144].rearrange("p (c t) -> p c t", c=4)
        vln = R[:, 6144:8192].rearrange("p (c t) -> p c t", c=4)
        attT = R[:, 8192:10240].rearrange("p (c t) -> p c t", c=4)
        sguT = R[:, 10240:12288].rearrange("p (c t) -> p c t", c=4)
        mrgT = R[:, 12288:16384].rearrange("p (c t) -> p c t", c=8)
        actT = R[:, 0:NFC * 512].rearrange("p (c t) -> p c t", c=NFC)
        RQ, RU, RV, RA, RS_, RM = "R_q", "R_u", "R_v", "R_a", "R_s", "R_m"
        R_ALL = [RQ, RU, RV, RA, RS_, RM]

        eng_sems = {e: es.enter_context(nc.semaphore("s_" + e)) for e in ENGS}
        dma_keys = ["c0", "c1", "c2"] + ["w%d" % i for i in range(NS)] + ["x%d" % i for i in range(4)] + \
                   ["o%d" % i for i in range(4)] + ["dbg"]
        dma_sems = {k: es.enter_context(nc.semaphore("d_" + k)) for k in dma_keys}
        block = es.enter_context(nc.Block())
        S = Sched()

        def pk(b):
            return [("ps", b)]

        S.op("sp", lambda e: e.dma_start(out=gcols[:], in_=gcols_d), writes=["gcols"], dma_sem="c0")
        S.op("sp", lambda e: e.dma_start(out=lng[:], in_=lng_d), writes=["lng"], dma_sem="c0")
        S.op("sp", lambda e: e.dma_start(out=lnb[:], in_=lnb_d), writes=["lnb"], dma_sem="c0")
        S.op("sp", lambda e: e.dma_start(out=bsb[:], in_=bsb_d), writes=["bsb"], dma_sem="c0")
        S.op("sp", lambda e: e.dma_start(out=bgate[:], in_=bgate_d), writes=["bgate"], dma_sem="c0")
        S.op("sp", lambda e: e.dma_start(out=maskT, in_=maskT_d), writes=[("sc", 1)], dma_sem="c0")
        S.op("sp", lambda e: e.dma_start(out=maskb, in_=maskb_d), writes=[("sc", 0)], dma_sem="c0")
        wsT_stage = gates[:].rearrange("p a b -> p (a b)")[:, 0:1024].rearrange("p (l g t) -> p l g t", l=2, g=4)
        S.op("sp", lambda e: e.dma_start(out=wsT_stage, in_=wsT_d), writes=[("gates", 0), ("gates", 1)], dma_sem="c0")
        S.dma_group_finalize("c0")
        S.op("pool", lambda e: e.dma_start(out=identb[:], in_=ident_d), writes=["identb"], dma_sem="c1")
        S.op("pool", lambda e: e.dma_start(out=vonesb[:], in_=vones_d), writes=["vonesb"], dma_sem="c1")
        S.dma_group_finalize("c1")
        S.op("dve", lambda e: e.memset(epsc[:], EPS), writes=["epsc"])
        S.op("dve", lambda e: e.memset(tinyc[:], 1e-20), writes=["tinyc"])
        S.op("dve", lambda e: e.tensor_tensor(out=wsTb[:].rearrange("p l g t -> p (l g) t"),
                                              in0=wsT_stage.rearrange("p l g t -> p (l g) t"),
                                              in1=maskT.unsqueeze(1).to_broadcast([128, 8, 128]), op=ALU.mult),
             reads=[("gates", 0), ("gates", 1), ("sc", 1)], writes=["wsTb"])
        Rf = R[:].bitcast(F32)
        bstage = Rf[:, 0:5120].rearrange("p (h c) -> p h c", h=8)
        for l in range(2):
            S.op("sp", (lambda l: lambda e: e.dma_start(out=bstage, in_=biasg_d[l]))(l), writes=R_ALL, dma_sem="c2")
            S.op("dve", (lambda l: lambda e: e.tensor_tensor(out=biasb[:, l, :, :], in0=bstage,
                                                             in1=maskb.unsqueeze(1).to_broadcast([128, 8, 640]),
                                                             op=ALU.add))(l),
                 reads=R_ALL + [("sc", 0)], writes=["biasb"])

        def units_for(l, kvonly):
            u = []

            def win(c0, n=512):
                return w_in[l][:, c0:c0 + n].rearrange("(c p) f -> p c f", p=128)

            def slot_view(kc, n):
                return (kc, n)

            if kvonly:
                u.append(("k", [((8, 512), win(512))]))
                u.append(("v", [((8, 512), win(1024))]))
                return u
            u.append(("vs", [((8, 512), win(2048))]))
            u.append(("u", [((8, 512), win(1536))]))
            u.append(("q", [((8, 512), win(0))]))
            u.append(("k", [((8, 512), win(512))]))
            u.append(("v", [((8, 512), win(1024))]))
            for hf in range(2):
                u.append(("br%d" % hf, [((4, 512, 0), w_bra[l][:, hf * 512:(hf + 1) * 512].rearrange("(c p) f -> p c f", p=128)),
                                        ((4, 512, 4), w_brs[l][:, hf * 512:(hf + 1) * 512].rearrange("(c p) f -> p c f", p=128))]))
                u.append(("g0%d" % hf, [((8, 512), win(2560 + hf * 512))]))
                u.append(("g1%d" % hf, [((8, 512), win(3584 + hf * 512))]))
            for hf in range(2):
                u.append(("wo%d" % hf, [((8, 512), w_out[l][:, hf * 512:(hf + 1) * 512].rearrange("(c p) f -> p c f", p=128))]))
            for i in range(6):
                n = 512 if i < 5 else 256
                u.append(("fg%d" % i, [((8, n), w_fi[l][:, i * 512:i * 512 + n].rearrange("(c p) f -> p c f", p=128))]))
                u.append(("fu%d" % i, [((8, n), w_fi[l][:, DFF + i * 512:DFF + i * 512 + n].rearrange("(c p) f -> p c f", p=128))]))
            for hf in range(2):
                for (c0, ncx) in ((0, 8), (8, 8), (16, 6)):
                    u.append(("fo%d_%d" % (hf, c0), [((ncx, 512), w_fo[l][c0 * 128:(c0 + ncx) * 128, hf * 512:(hf + 1) * 512]
                                                       .rearrange("(c p) f -> p c f", p=128))]))
            return u

        passes = []
        for s in range(n_sb):
            if s == 0:
                passes.append((0, s, True))
            else:
                passes.append((0, s, False))
                if n_layers > 1:
                    passes.append((1, s, s == 1))
        all_units = []
        for (l, s, kvonly) in passes:
            for (kind, dmas) in units_for(l, kvonly):
                all_units.append((l, s, kind, dmas))
        wstate = {"next_issue": 0, "next_use": 0}

        def issue_unit(n):
            l, s, kind, dmas = all_units[n]
            slot = n % NS
            for (spec, src) in dmas:
                if len(spec) == 2:
                    kc, ncol = spec
                    k0 = 0
                else:
                    kc, ncol, k0 = spec
                dst = Wr[:, slot, :].rearrange("p (c f) -> p c f", c=8)[:, k0:k0 + kc, 0:ncol] if ncol == 512 else \
                    Wr[:, slot, 0:8 * ncol].rearrange("p (c f) -> p c f", c=8)[:, k0:k0 + kc, :]
                S.op("pool", (lambda dst, src: lambda e: e.dma_start(out=dst, in_=src))(dst, src),
                     writes=[("w", slot, 0), ("w", slot, 1)], dma_sem="w%d" % slot)
            S.dma_group_finalize("w%d" % slot)

        def w_prefetch():
            while wstate["next_issue"] < len(all_units) and wstate["next_issue"] < wstate["next_use"] + NS:
                issue_unit(wstate["next_issue"])
                wstate["next_issue"] += 1

        def w_acquire(l, s, kind):
            n = wstate["next_use"]
            ul, us, ukind, dmas = all_units[n]
            assert (ul, us, ukind) == (l, s, kind), ((ul, us, ukind), (l, s, kind))
            assert wstate["next_issue"] > n
            wstate["next_use"] += 1
            slot = n % NS
            ncol = dmas[0][0][1]
            view = Wr[:, slot, 0:8 * ncol].rearrange("p (c f) -> p c f", c=8)
            return view, [("w", slot, 0), ("w", slot, 1)]

        def w_release():
            w_prefetch()

        w_prefetch()

        def blk_slot(b):
            return b % 8

        def load_x(s):
            for j in range(4):
                b = 4 * s + j
                S.op("sp", (lambda b, j: lambda e: e.dma_start(out=h[:, j, :], in_=xs[b * 128:(b + 1) * 128, :]))(b, j),
                     writes=[("h", j)], dma_sem="x%d" % j)

        tp_bank = [6]

        def norm_stats(j):
            S.op("act", lambda e: e.activation(out=junk, in_=h[:, j, :], func=AF.Square, accum_out=ss[:, j:j + 1]),
                 reads=[("h", j)], writes=JUNK + [("ss", j)])
            S.op("act", lambda e: e.activation(out=rstd[:, j:j + 1], in_=ss[:, j:j + 1], func=AF.Sqrt, scale=1.0 / D, bias=epsc[:]),
                 reads=[("ss", j), "epsc"], writes=[("rstd", j)])
            S.op("dve", lambda e: e.reciprocal(out=rstd[:, j:j + 1], in_=rstd[:, j:j + 1]), reads=[("rstd", j)], writes=[("rstd", j)])

        def stage_norm(gi):
            for j in range(4):
                norm_stats(j)
            banks = []

            def scale_and_transpose(j):
                xb = j % 2
                if j % 2 == 0:
                    S.op("dve", lambda e: e.tensor_scalar(out=xn_tm[:, xb, :], in0=h[:, j, :], scalar1=rstd[:, j:j + 1],
                                                          scalar2=None, op0=ALU.mult),
                         reads=[("h", j), ("rstd", j)], writes=[("xn_tm", xb)])
                else:
                    S.op("act", lambda e: e.activation(out=xn_tm[:, xb, :], in_=h[:, j, :], func=AF.Copy, scale=rstd[:, j:j + 1]),
                         reads=[("h", j), ("rstd", j)], writes=[("xn_tm", xb)])
                bank = tp_bank[0]
                tp_bank[0] = 6 if bank == 7 else 7
                pT = ps[:, bank, :].bitcast(BF16).rearrange("p (c t) -> p c t", c=8)
                for c in range(8):
                    S.op("pe", (lambda c, pT: lambda e: e.transpose(out=pT[:, c, :], in_=xn_tm[:, xb, c * 128:(c + 1) * 128],
                                                                     identity=identb[:]))(c, pT),
                         reads=[("xn_tm", xb), "identb"], writes=pk(bank))
                banks.append((bank, pT))

            def evac(j):
                bank, pT = banks[j]
                S.op("dve", lambda e: e.tensor_tensor(
                    out=xnT[:, :, j * 128:(j + 1) * 128], in0=pT,
                    in1=gcols[:, gi, :].unsqueeze(2).to_broadcast([128, 8, 128]), op=ALU.mult),
                    reads=pk(bank) + ["gcols"], writes=[("xnT", j)])

            scale_and_transpose(0)
            for j in range(1, 4):
                scale_and_transpose(j)
                evac(j - 1)
            evac(3)

        XNT_ALL = [("xnT", j) for j in range(4)]
        acc_bank = [0]

        def next_bank(lo=0, n=4):
            b = lo + acc_bank[0] % n
            acc_bank[0] += 1
            return b

        def proj_fm(l, s, kind, evac):
            wv, wkey = w_acquire(l, s, kind)
            for fc in range(4):
                bank = next_bank()
                for k in range(8):
                    S.op("pe", (lambda fc, k, bank: lambda e: e.matmul(ps[:, bank, :], lhsT=wv[:, k, fc * 128:(fc + 1) * 128],
                                                                       rhs=xnT[:, k, :], start=(k == 0), stop=(k == 7)))(fc, k, bank),
                         reads=wkey + XNT_ALL, writes=pk(bank))
                evac(fc, bank)
            w_release()

        def proj_tm(l, s, kind, evac):
            wv, wkey = w_acquire(l, s, kind)
            for j in range(4):
                bank = next_bank()
                for k in range(8):
                    S.op("pe", (lambda j, k, bank: lambda e: e.matmul(ps[:, bank, :], lhsT=xnT[:, k, j * 128:(j + 1) * 128],
                                                                      rhs=wv[:, k, :], start=(k == 0), stop=(k == 7)))(j, k, bank),
                         reads=wkey + [("xnT", j)], writes=pk(bank))
                evac(j, bank)
            w_release()

        def stage_proj(l, s, kvonly):
            koff = ((4 * s) % 8) * 128

            def ev_q(fc, bank):
                if fc == 0:
                    S.op("pool", lambda e: e.memset(Qz[64:128, 0:8:2, :], 0.0), writes=[RQ])
                    S.op("pool", lambda e: e.memset(Qz[0:64, 1:8:2, :], 0.0), writes=[RQ])
                S.op("act", lambda e: e.activation(out=Qz[0:64, 2 * fc, :], in_=ps[0:64, bank, :], func=AF.Copy, scale=0.125),
                     reads=pk(bank), writes=[RQ, ("Qh", 2 * fc)])
                S.op("act", lambda e: e.activation(out=Qz[64:128, 2 * fc + 1, :], in_=ps[64:128, bank, :], func=AF.Copy, scale=0.125),
                     reads=pk(bank), writes=[RQ, ("Qh", 2 * fc + 1)])

            def ev_k(fc, bank):
                S.op("dve", lambda e: e.tensor_copy(out=Kr[l][:, fc, koff:koff + 512], in_=ps[:, bank, :]),
                     reads=pk(bank), writes=[("Kr", l, (4 * s) % 8 // 4)])

            def ev_v(j, bank):
                slot = blk_slot(4 * s + j)
                S.op("dve", lambda e: e.tensor_copy(out=Vr[l][:, slot, :], in_=ps[:, bank, :]),
                     reads=pk(bank), writes=[("Vr", l, slot)])

            def ev_u(fc, bank):
                S.op("act", lambda e: e.activation(out=uT[:, fc, :], in_=ps[:, bank, :], func=AF.Gelu_apprx_tanh),
                     reads=pk(bank), writes=[RU])

            def ev_vs(j, bank):
                S.op("act", lambda e: e.activation(out=gv[:, j, :], in_=ps[:, bank, :], func=AF.Gelu_apprx_tanh),
                     reads=pk(bank), writes=[("gv", j)])
                S.op("dve", lambda e: e.bn_stats(out=stats[:, j, :], in_=gv[:, j, :]), reads=[("gv", j)], writes=[("stats", j)])
                S.op("dve", lambda e: e.bn_aggr(out=mv[:, j, :], in_=stats[:, j, :]), reads=[("stats", j)], writes=[("mv", j)])

            if kvonly:
                proj_fm(l, s, "k", ev_k)
                proj_tm(l, s, "v", ev_v)
                return
            proj_tm(l, s, "vs", ev_vs)
            proj_fm(l, s, "u", ev_u)
            S.op("act", lambda e: e.activation(out=lrs[:], in_=mv[:, :, 1], func=AF.Sqrt, scale=1.0, bias=epsc[:]),
                 reads=[("mv", j) for j in range(4)] + ["epsc"], writes=["lrs"])
            S.op("dve", lambda e: e.reciprocal(out=lrs[:], in_=lrs[:]), reads=["lrs"], writes=["lrs"])
            for j in range(4):
                S.op("dve", (lambda j: lambda e: e.tensor_scalar(out=gv[:, j, :], in0=gv[:, j, :], scalar1=mv[:, j, 0:1],
                                                                  scalar2=lrs[:, j:j + 1], op0=ALU.subtract, op1=ALU.mult))(j),
                     reads=[("gv", j), ("mv", j), "lrs"], writes=[("gv", j)])
                S.op("dve", (lambda j: lambda e: e.tensor_tensor(out=gv[:, j, :], in0=gv[:, j, :], in1=lng[:, l, :], op=ALU.mult))(j),
                     reads=[("gv", j), "lng"], writes=[("gv", j)])
                S.op("dve", (lambda j: lambda e: e.tensor_tensor(out=vln[:, j, :], in0=gv[:, j, :], in1=lnb[:, l, :], op=ALU.add))(j),
                     reads=[("gv", j), "lnb"], writes=[RV])
            proj_fm(l, s, "q", ev_q)
            proj_fm(l, s, "k", ev_k)
            proj_tm(l, s, "v", ev_v)

        sbuf_ctr = {"sA": 0, "sc": 0, "pt": 0, "dt": 0}

        def stage_attn(l, s):
            units = [(j, hh) for j in range(4) for hh in range(8)]
            st = {}

            def emit_qk(n):
                j, hh = units[n]
                p = hh // 2
                b = 4 * s + j
                kslots = [blk_slot(b - 4 + i) for i in range(5)]
                kreads = [("Kr", l, ks // 4) for ks in set(kslots)]
                sbi = sbuf_ctr["sA"] % 3
                sbuf_ctr["sA"] += 1
                base = sbi * 1024
                banks = [2 * sbi, 2 * sbi + 1]
                for i in range(5):
                    ks = kslots[i]
                    col = base + i * 128
                    S.op("pe", (lambda col, ks, p, hh, j: lambda e: e.matmul(
                        psflat[:, col:col + 128], lhsT=Kr[l][:, p, ks * 128:(ks + 1) * 128],
                        rhs=Qz[:, hh, j * 128:(j + 1) * 128], start=True, stop=False))(col, ks, p, hh, j),
                        reads=kreads + [RQ, ("Qh", hh)], writes=pk(col // 512))
                    S.op("pe", (lambda col, hh, i: lambda e: e.matmul(
                        psflat[:, col:col + 128], lhsT=identb[:], rhs=biasb[:, l, hh, i * 128:(i + 1) * 128],
                        start=False, stop=True))(col, hh, i),
                        reads=["identb", "biasb"], writes=pk(col // 512))
                pi = sbuf_ctr["pt"] % 3
                sbuf_ctr["pt"] += 1
                S.op("act", (lambda base, pi: lambda e: e.activation(out=PT[:, pi, :], in_=psflat[:, base:base + 640],
                                                                     func=AF.Exp))(base, pi),
                     reads=pk(banks[0]) + pk(banks[1]), writes=[("PT", pi)])
                st[n] = (kslots, pi)

            def emit_pv(n):
                j, hh = units[n]
                p, par = hh // 2, hh % 2
                b = 4 * s + j
                kslots, pi = st.pop(n)
                bank = 6 + p % 2
                for kind in ("num", "den"):
                    cc = par * 128 + (0 if kind == "num" else 256)
                    for i in range(5):
                        ks = kslots[i]
                        if kind == "num":
                            lhs = Vr[l][:, ks, p * 128:(p + 1) * 128]
                            rd = [("Vr", l, ks)]
                        else:
                            lhs = vonesb[:, b - 4 + i, :]
                            rd = ["vonesb"]
                        S.op("pe", (lambda lhs, pi, i, bank, cc: lambda e: e.matmul(
                            ps[:, bank, cc:cc + 128], lhsT=lhs,
                            rhs=PT[:, pi, i * 128:(i + 1) * 128], start=(i == 0), stop=(i == 4)))(lhs, pi, i, bank, cc),
                            reads=rd + [("PT", pi)], writes=pk(bank))
                if par != 1:
                    return None

                di = sbuf_ctr["dt"] % 2
                sbuf_ctr["dt"] += 1
                dsl = dtmp[:, di * 256:(di + 1) * 256]

                def norm_act():
                    S.op("act", (lambda bank, dsl: lambda e: e.activation(out=dsl, in_=ps[:, bank, 256:512], func=AF.Ln,
                                                                          bias=tinyc[:]))(bank, dsl),
                         reads=pk(bank) + ["tinyc"], writes=[("dtmp", di)])
                    S.op("act", (lambda dsl: lambda e: e.activation(out=dsl, in_=dsl, func=AF.Exp, scale=-1.0))(dsl),
                         reads=[("dtmp", di)], writes=[("dtmp", di)])

                def norm_dve():
                    for hf2 in range(2):
                        rs = slice(hf2 * 64, (hf2 + 1) * 64)
                        S.op("dve", (lambda j, p, bank, dsl, rs, hf2: lambda e: e.tensor_tensor(
                            out=attT[rs, p, j * 128:(j + 1) * 128], in0=ps[rs, bank, hf2 * 128:(hf2 + 1) * 128],
                            in1=dsl[rs, hf2 * 128:(hf2 + 1) * 128], op=ALU.mult))(j, p, bank, dsl, rs, hf2),
                            reads=pk(bank) + [("dtmp", di)], writes=[RA])
                return (norm_act, norm_dve)

            nu = len(units)
            pend_act, pend_dve = [], []
            for n in range(min(ATT_SKEW, nu)):
                emit_qk(n)
            for n in range(nu):
                if n + ATT_SKEW < nu:
                    emit_qk(n + ATT_SKEW)
                if n % 2 == 1 and pend_act:
                    fa, fd = pend_act.pop(0)
                    fa()
                    pend_dve.append(fd)
                if n % 2 == 0 and pend_dve:
                    pend_dve.pop(0)()
                f = emit_pv(n)
                if f is not None:
                    pend_act.append(f)
            while pend_dve:
                pend_dve.pop(0)()
            while pend_act:
                fa, fd = pend_act.pop(0)
                fa()
                fd()

        def stage_sgu(l, s):
            for j in range(4):
                bank = next_bank()
                for g in range(4):
                    S.op("pe", (lambda j, g, bank: lambda e: e.matmul(ps[:, bank, g * 128:(g + 1) * 128],
                                                                      lhsT=vln[:, j, g * 128:(g + 1) * 128],
                                                                      rhs=wsTb[:, l, g, :], start=True, stop=True))(j, g, bank),
                         reads=[RV, "wsTb"], writes=pk(bank))
                gb = gates[:, j % 2, :]
                S.op("dve", (lambda bank, gb: lambda e: e.tensor_tensor(out=gb, in0=ps[:, bank, :], in1=bsb[:, l, :], op=ALU.add))(bank, gb),
                     reads=pk(bank) + ["bsb"], writes=[("gates", j % 2)])
                S.op("dve", (lambda j, gb: lambda e: e.tensor_tensor(
                    out=sguT[:, :, j * 128:(j + 1) * 128], in0=gb.rearrange("p (c t) -> p c t", c=4),
                    in1=uT[:, :, j * 128:(j + 1) * 128], op=ALU.mult))(j, gb),
                    reads=[("gates", j % 2), RU], writes=[RS_])

        def stage_merge(l, s):
            for hf in range(2):
                wbr, kbr = w_acquire(l, s, "br%d" % hf)
                wg0, kg0 = w_acquire(l, s, "g0%d" % hf)
                wg1, kg1 = w_acquire(l, s, "g1%d" % hf)
                for f4 in range(4):
                    fc = hf * 4 + f4
                    bset = (fc % 2) * 4
                    bA, bB, bG0, bG1 = bset, bset + 1, bset + 2, bset + 3
                    cs = slice(f4 * 128, (f4 + 1) * 128)
                    for k in range(4):
                        S.op("pe", (lambda k, bA, cs, wbr: lambda e: e.matmul(ps[:, bA, :], lhsT=wbr[:, k, cs], rhs=attT[:, k, :],
                                                                              start=(k == 0), stop=(k == 3)))(k, bA, cs, wbr),
                             reads=kbr + [RA], writes=pk(bA))
                    for k in range(4):
                        S.op("pe", (lambda k, bB, cs, wbr: lambda e: e.matmul(ps[:, bB, :], lhsT=wbr[:, 4 + k, cs], rhs=sguT[:, k, :],
                                                                              start=(k == 0), stop=(k == 3)))(k, bB, cs, wbr),
                             reads=kbr + [RS_], writes=pk(bB))
                    for (wg, kg, bG, gi) in ((wg0, kg0, bG0, 0), (wg1, kg1, bG1, 1)):
                        for k in range(8):
                            S.op("pe", (lambda k, bG, cs, wg: lambda e: e.matmul(ps[:, bG, :], lhsT=wg[:, k, cs], rhs=xnT[:, k, :],
                                                                                 start=(k == 0), stop=(k == 7)))(k, bG, cs, wg),
                                 reads=kg + XNT_ALL, writes=pk(bG))
                        gidx = (fc % 2) * 2 + gi
                        S.op("act", (lambda bG, gidx, gi, fc: lambda e: e.activation(
                            out=gates[:, gidx, :], in_=ps[:, bG, :], func=AF.Sigmoid, bias=bgate[:, l, gi, fc:fc + 1]))(bG, gidx, gi, fc),
                            reads=pk(bG) + ["bgate"], writes=[("gates", gidx)])
                    g0i, g1i = (fc % 2) * 2, (fc % 2) * 2 + 1
                    S.op("dve", (lambda bA, g0i: lambda e: e.tensor_tensor(out=gates[:, g0i, :], in0=ps[:, bA, :], in1=gates[:, g0i, :],
                                                                           op=ALU.mult))(bA, g0i),
                         reads=pk(bA) + [("gates", g0i)], writes=[("gates", g0i)])
                    S.op("dve", (lambda bB, g1i: lambda e: e.tensor_tensor(out=gates[:, g1i, :], in0=ps[:, bB, :], in1=gates[:, g1i, :],
                                                                           op=ALU.mult))(bB, g1i),
                         reads=pk(bB) + [("gates", g1i)], writes=[("gates", g1i)])
                    S.op("dve", (lambda fc, g0i, g1i: lambda e: e.tensor_tensor(out=mrgT[:, fc, :], in0=gates[:, g0i, :],
                                                                                in1=gates[:, g1i, :], op=ALU.add))(fc, g0i, g1i),
                         reads=[("gates", g0i), ("gates", g1i)], writes=[RM])
                w_release()
                w_release()
                w_release()

        def stage_wout(l, s):
            for hf in range(2):
                wv, wkey = w_acquire(l, s, "wo%d" % hf)
                for j in range(4):
                    bank = next_bank()
                    for k in range(8):
                        S.op("pe", (lambda j, k, bank, wv: lambda e: e.matmul(ps[:, bank, :], lhsT=mrgT[:, k, j * 128:(j + 1) * 128],
                                                                              rhs=wv[:, k, :], start=(k == 0), stop=(k == 7)))(j, k, bank, wv),
                             reads=wkey + [RM], writes=pk(bank))
                    S.op("dve", (lambda j, bank, hf: lambda e: e.tensor_tensor(out=h[:, j, hf * 512:(hf + 1) * 512], in0=ps[:, bank, :],
                                                                               in1=h[:, j, hf * 512:(hf + 1) * 512], op=ALU.add))(j, bank, hf),
                         reads=pk(bank) + [("h", j)], writes=[("h", j)])
                w_release()

        def stage_ffn_in(l, s):
            for i in range(6):
                wg, kg = w_acquire(l, s, "fg%d" % i)
                wu, ku = w_acquire(l, s, "fu%d" % i)
                nch = 4 if i < 5 else 2
                for c4 in range(nch):
                    c = i * 4 + c4
                    bset = (c % 4) * 2
                    bG, bU = bset, bset + 1
                    cs = slice(c4 * 128, (c4 + 1) * 128)
                    for (wv, wk, bank) in ((wg, kg, bG), (wu, ku, bU)):
                        for k in range(8):
                            S.op("pe", (lambda k, bank, cs, wv: lambda e: e.matmul(ps[:, bank, :], lhsT=wv[:, k, cs], rhs=xnT[:, k, :],
                                                                                   start=(k == 0), stop=(k == 7)))(k, bank, cs, wv),
                                 reads=wk + XNT_ALL, writes=pk(bank))
                    si = c % 2
                    S.op("act", (lambda bG, si: lambda e: e.activation(out=sg[:, si, :], in_=ps[:, bG, :], func=AF.Silu))(bG, si),
                         reads=pk(bG), writes=[("gv", si)])
                    S.op("dve", (lambda bU, si, c: lambda e: e.tensor_tensor(out=actT[:, c, :], in0=ps[:, bU, :], in1=sg[:, si, :],
                                                                             op=ALU.mult))(bU, si, c),
                         reads=pk(bU) + [("gv", si)], writes=R_ALL)
                w_release()
                w_release()

        def stage_ffn_out(l, s):
            for hf in range(2):
                ws = []
                for (c0, ncx) in ((0, 8), (8, 8), (16, 6)):
                    wv, wkey = w_acquire(l, s, "fo%d_%d" % (hf, c0))
                    ws.append((c0, ncx, wv, wkey))
                for j in range(4):
                    bank = next_bank()
                    for (c0, ncx, wv, wkey) in ws:
                        for cc in range(ncx):
                            c = c0 + cc
                            S.op("pe", (lambda j, c, cc, bank, wv: lambda e: e.matmul(
                                ps[:, bank, :], lhsT=actT[:, c, j * 128:(j + 1) * 128], rhs=wv[:, cc, :],
                                start=(c == 0), stop=(c == NFC - 1)))(j, c, cc, bank, wv),
                                reads=wkey + R_ALL, writes=pk(bank))
                    S.op("dve", (lambda j, bank, hf: lambda e: e.tensor_tensor(out=h[:, j, hf * 512:(hf + 1) * 512], in0=ps[:, bank, :],
                                                                               in1=h[:, j, hf * 512:(hf + 1) * 512], op=ALU.add))(j, bank, hf),
                         reads=pk(bank) + [("h", j)], writes=[("h", j)])
                w_release()
                w_release()
                w_release()

        def stage_final(s):
            S.op("sp", lambda e: e.dma_start(out=gfin, in_=gfin_d), writes=[("gv", 2), ("gv", 3)], dma_sem="c0")
            S.dma_group_finalize("c0")
            yst = R[:].bitcast(F32)[:, 0:4 * D].rearrange("p (j f) -> p j f", j=4)
            for j in range(4):
                norm_stats(j)
                S.op("dve", (lambda j: lambda e: e.scalar_tensor_tensor(
                    out=yst[:, j, :], in0=h[:, j, :], scalar=rstd[:, j:j + 1], in1=gfin, op0=ALU.mult, op1=ALU.mult))(j),
                    reads=[("h", j), ("rstd", j), ("gv", 2), ("gv", 3)], writes=(R_ALL if j == 0 else []) + [("yst", j)])
                ob = (4 * s + j) - 8
                S.op("sp", (lambda j, ob: lambda e: e.dma_start(out=y[ob * 128:(ob + 1) * 128, :], in_=yst[:, j, :]))(j, ob),
                     reads=R_ALL + [("yst", j)], dma_sem="o%d" % j)

        gfin_d = dt_in("gfin", [128, D])

        dstage = sb("dstage", [128, D], F32) if DEBUG else None
        dctr = [0]

        def dump(name, ap, reads, n=D):
            if not DEBUG or name not in dumps:
                return
            idx = dctr[0]
            dctr[0] += 1
            _LAST_DUMP_ORDER.append(name)
            S.op("dve", lambda e: e.tensor_copy(out=dstage[:, 0:n], in_=ap), reads=reads, writes=["dstage"])
            S.op("sp", lambda e: e.dma_start(out=dbg[idx][:, 0:n], in_=dstage[:, 0:n]), reads=["dstage"], dma_sem="dbg")

        def chk(tag):
            if stop_after == tag:
                raise _Stop()

        def run_layer(l, s, kvonly):
            T = "L%d_S%d_" % (l, s)
            stage_norm(2 * l)
            dump(T + "xnT0", xnT[:, 0, :], XNT_ALL, 512)
            dump(T + "xnT7", xnT[:, 7, :], XNT_ALL, 512)
            chk(T + "norm")
            stage_proj(l, s, kvonly)
            dump(T + "K0", Kr[l][:, 0, :], [("Kr", l, 0), ("Kr", l, 1)], 1024)
            dump(T + "V0", Vr[l][:, blk_slot(4 * s), :], [("Vr", l, blk_slot(4 * s))], 512)
            if kvonly:
                chk(T + "proj")
                return
            dump(T + "q0", Qz[:, 0, :], [RQ, ("Qh", 0)], 512)
            dump(T + "u0", uT[:, 0, :], [RU], 512)
            dump(T + "vln0", vln[:, 0, :], [RV], 512)
            chk(T + "proj")
            stage_attn(l, s)
            dump(T + "att0", attT[:, 0, :], [RA], 512)
            dump(T + "att3", attT[:, 3, :], [RA], 512)
            chk(T + "attn")
            stage_sgu(l, s)
            dump(T + "sgu0", sguT[:, 0, :], [RS_], 512)
            chk(T + "sgu")
            stage_merge(l, s)
            for fcx in range(8):
                dump(T + "mrg%d" % fcx, mrgT[:, fcx, :], [RM], 512)
            chk(T + "merge")
            stage_wout(l, s)
            dump(T + "h0mid", h[:, 0, :], [("h", 0)])
            chk(T + "wout")
            stage_norm(2 * l + 1)
            stage_ffn_in(l, s)
            dump(T + "act0", actT[:, 0, :], R_ALL, 512)
            dump(T + "act21", actT[:, 21, :], R_ALL, 512)
            chk(T + "ffn_in")
            stage_ffn_out(l, s)
            dump(T + "h0", h[:, 0, :], [("h", 0)])
            dump(T + "h3", h[:, 3, :], [("h", 3)])
            chk(T + "ffn_out")

        try:
            for s in range(n_sb):
                load_x(s)
                run_layer(0, s, s == 0)
                if s >= 1 and n_layers > 1:
                    run_layer(1, s, s == 1)
                if s >= 2:
                    stage_final(s)
            assert wstate["next_use"] == len(all_units), (wstate, len(all_units))
        except _Stop:
            pass

        S.final_waits["sp"] = [("o%d" % j, S.dma_counts.get("o%d" % j, 0)) for j in range(4)]
        S.final_waits["pool"] = [(k, v) for k, v in S.dma_counts.items() if k.startswith("w") or k == "c1"]
        S.final_waits["sp"] += [(k, v) for k, v in S.dma_counts.items() if k.startswith("x") or k in ("c0", "c2")]
        if DEBUG:
            S.final_waits["sp"].append(("dbg", S.dma_counts.get("dbg", 0)))
        S.emit(block, eng_sems, dma_sems)
    return nc


def _host_consts(att_rel_bias, sgu_norm_gain, sgu_norm_bias, sgu_w, sgu_b, b_gate, norm_mix, norm_ffn, norm_final):
    f32 = np.float32
    gl = [norm_mix[0], norm_ffn[0], norm_mix[1], norm_ffn[1], norm_final]
    gcols = np.stack([np.asarray(g, f32).reshape(8, 128).T for g in gl], axis=1)
    lng = np.broadcast_to(np.asarray(sgu_norm_gain, f32)[None], (128, 2, 512)).copy()
    lnb = np.broadcast_to(np.asarray(sgu_norm_bias, f32)[None], (128, 2, 512)).copy()
    bsb = np.broadcast_to(np.asarray(sgu_b, f32).reshape(2, 512)[None], (128, 2, 512)).copy()
    wsT = np.ascontiguousarray(np.asarray(sgu_w, f32).transpose(3, 0, 1, 2))
    maskT = (np.arange(128)[:, None] <= np.arange(128)[None, :]).astype(f32)
    ki = np.arange(128)[:, None, None]
    i5 = np.arange(5)[None, :, None]
    qi = np.arange(128)[None, None, :]
    rel = np.clip((4 - i5) * 128 + qi - ki, -128, 128) + 128
    biasg = np.asarray(att_rel_bias, f32)[:, :, rel]
    biasg = np.ascontiguousarray(biasg.transpose(0, 2, 1, 3, 4)).reshape(2, 128, 8, 640)
    dchunk = (8 - 2 * i5) + (qi // 64) - (ki // 64)
    valid = (dchunk >= 0) & (dchunk <= 8)
    maskb = np.where(valid, 0.0, NEG).astype(f32).reshape(128, 640)
    bgate = np.ascontiguousarray(np.asarray(b_gate, f32).reshape(2, 2, 8, 128).transpose(3, 0, 1, 2))
    gfin = np.broadcast_to(np.asarray(norm_final, f32)[None], (128, D)).copy()
    ident = np.eye(128, dtype=f32)
    return dict(gcols=gcols, lng=lng, lnb=lnb, bsb=bsb, wsT=wsT, maskT=maskT, biasg=biasg, maskb=maskb,
                bgate=bgate, gfin=gfin, ident=ident)


def _make_in_maps(x, shared):
    f32 = np.float32
    in_maps = []
    for c in range(8):
        b, half = c // 2, c % 2
        xs = np.zeros((NBLK * 128, D), f32)
        vones = np.ones((128, NBLK, 128), f32)
        if half == 0:
            xs[1024:] = x[b, 0:2048]
            vones[:, 0:8, :] = 0.0
        else:
            xs[:] = x[b, 1024:4096]
        m = dict(shared)
        m["xs"] = xs
        m["vones"] = vones
        in_maps.append(m)
    return in_maps


_PROGRAM = {}


def kernel(x, norm_mix, w_in, att_rel_bias, sgu_norm_gain, sgu_norm_bias, sgu_w, sgu_b,
           w_br_att, w_br_sgu, b_gate, w_out, norm_ffn, w_ffn_in, w_ffn_out, norm_final):
    f32 = np.float32
    x = np.asarray(x, f32)
    consts = _host_consts(att_rel_bias, sgu_norm_gain, sgu_norm_bias, sgu_w, sgu_b, b_gate, norm_mix, norm_ffn, norm_final)
    shared = dict(w_in=np.ascontiguousarray(w_in, f32), w_br_att=np.ascontiguousarray(w_br_att, f32),
                  w_br_sgu=np.ascontiguousarray(w_br_sgu, f32), w_out=np.ascontiguousarray(w_out, f32),
                  w_ffn_in=np.ascontiguousarray(w_ffn_in, f32), w_ffn_out=np.ascontiguousarray(w_ffn_out, f32))
    shared.update(consts)
    in_maps = _make_in_maps(x, shared)
    if "nc" not in _PROGRAM:
        _PROGRAM["nc"] = build_program()
    res = run_bass_kernel_spmd(_PROGRAM["nc"], in_maps, core_ids=list(range(8)))
    out = np.empty((4, 4096, D), f32)
    for c in range(8):
        b, half = c // 2, c % 2
        out[b, half * 2048:(half + 1) * 2048] = res.results[c]["y"]
    return out
```
